# Optimizing a Trainium2 kernel written in Bass

```python
import math
import jax, jax.numpy as jnp
from jax import lax
import numpy as np

D_MODEL = 1024
BATCH = 16
SEQ = 2048
DEPTH = 1

HEAD_DIM = 64
GROUPS = ((128, 1), (512, 4), (2048, 16))
N_GROUPS = len(GROUPS)
HEADS_PER_GROUP = 8
N_HEADS = N_GROUPS * HEADS_PER_GROUP
ATTN_WIDTH = N_HEADS * HEAD_DIM
ATTN_OUT_WIDTH = HEADS_PER_GROUP * HEAD_DIM
Q_BLOCK = 128
CONV_WIDTH = D_MODEL
CONV_KERNEL = 31
N_BRANCHES = 2
D_FF = -(-8 * D_MODEL // (3 * 256)) * 256
IN_WIDTH = 3 * ATTN_WIDTH + 2 * CONV_WIDTH + N_BRANCHES * D_MODEL
RMS_EPS = 1e-6
LN_EPS = 1e-5

kernel_name = "hybrid_dilated_attn_conformer_conv_gated"


def _alibi_slope_list(n):
    def pow2(m):
        start = 2.0 ** (-8.0 / m)
        return [start ** (i + 1) for i in range(m)]
    if math.log2(n).is_integer():
        return pow2(n)
    c = 2 ** math.floor(math.log2(n))
    return pow2(c) + _alibi_slope_list(2 * c)[0::2][: n - c]


def _alibi_slopes():
    s = sorted(_alibi_slope_list(N_HEADS), reverse=True)
    return np.asarray(s, dtype=np.float32).reshape(N_GROUPS, HEADS_PER_GROUP)


def _rmsnorm(x, g):
    x32 = x.astype(jnp.float32)
    y = x32 * lax.rsqrt(jnp.mean(x32 * x32, axis=-1, keepdims=True) + RMS_EPS)
    return (y * g.astype(jnp.float32)).astype(x.dtype)


def _layernorm(x, g, b):
    x32 = x.astype(jnp.float32)
    mu = jnp.mean(x32, axis=-1, keepdims=True)
    var = jnp.mean(jnp.square(x32 - mu), axis=-1, keepdims=True)
    y = (x32 - mu) * lax.rsqrt(var + LN_EPS)
    return (y * g.astype(jnp.float32) + b.astype(jnp.float32)).astype(x.dtype)


def _dilated_group(q, k, v, slopes, window, dilation):
    B, S, Hg, hd = q.shape
    r = dilation
    L = S // r
    n_back = window // r
    assert n_back <= Q_BLOCK
    nb = -(-L // Q_BLOCK)
    Lp = nb * Q_BLOCK

    def to_sub(t):
        return t.reshape(B, L, r, Hg, hd).transpose(0, 2, 3, 1, 4)

    qb = jnp.pad(to_sub(q), ((0, 0), (0, 0), (0, 0), (0, Lp - L), (0, 0)))
    qb = qb.reshape(B, r, Hg, nb, Q_BLOCK, hd)

    def band(t):
        t = jnp.pad(to_sub(t), ((0, 0), (0, 0), (0, 0), (Q_BLOCK, Lp - L), (0, 0)))
        t = t.reshape(B, r, Hg, nb + 1, Q_BLOCK, hd)
        return jnp.concatenate([t[:, :, :, :-1], t[:, :, :, 1:]], axis=4)

    kb, vb = band(k), band(v)
    qi = jnp.arange(Q_BLOCK)[:, None]
    kj = jnp.arange(2 * Q_BLOCK)[None, :]
    rel = Q_BLOCK + qi - kj
    kpos = (jnp.arange(nb)[:, None, None] - 1) * Q_BLOCK + kj[None]
    valid = (rel >= 0) & (rel <= n_back) & (kpos >= 0)
    dist = (rel * r).astype(jnp.float32)

    s = jnp.einsum('brhnqd,brhnkd->brhnqk', qb, kb).astype(jnp.float32) * (hd ** -0.5)
    s = s - slopes.astype(jnp.float32)[:, None, None, None] * dist
    s = jnp.where(valid, s, -jnp.inf)
    m = jnp.max(s, axis=-1, keepdims=True)
    p = jnp.exp(s - m)
    denom = jnp.sum(p, axis=-1)
    o = jnp.einsum('brhnqk,brhnkd->brhnqd', p, vb.astype(jnp.float32)) / denom[..., None]
    lse = m[..., 0] + jnp.log(denom)

    o = o.reshape(B, r, Hg, Lp, hd)[:, :, :, :L].transpose(0, 3, 1, 2, 4).reshape(B, S, Hg, hd)
    lse = lse.reshape(B, r, Hg, Lp)[:, :, :, :L].transpose(0, 3, 1, 2).reshape(B, S, Hg)
    return o, lse


def _causal_depthwise_conv(u, w, b):
    C = u.shape[-1]
    y = lax.conv_general_dilated(
        u, w[:, None, :], window_strides=(1,), padding=[(CONV_KERNEL - 1, 0)],
        dimension_numbers=('NWC', 'WIO', 'NWC'), feature_group_count=C)
    return y + b


def setup_inputs(seed: int = 0) -> dict:
    key = jax.random.key(seed)
    ks = jax.random.split(key, 17)
    f32 = jnp.float32

    def w(k, shape, fan_in):
        return jax.random.normal(k, shape, f32) * (fan_in ** -0.5)

    def gain(k, shape):
        return 1.0 + 0.05 * jax.random.normal(k, shape, f32)

    D = DEPTH
    return {
        "x": jax.random.normal(ks[0], (BATCH, SEQ, D_MODEL), f32),
        "norm1_g": gain(ks[1], (D, D_MODEL)),
        "w_in": w(ks[2], (D, D_MODEL, IN_WIDTH), D_MODEL),
        "gate_b": 0.1 * jax.random.normal(ks[3], (D, N_BRANCHES * D_MODEL), f32),
        "conv_w": w(ks[4], (D, CONV_KERNEL, CONV_WIDTH), CONV_KERNEL),
        "conv_b": 0.02 * jax.random.normal(ks[5], (D, CONV_WIDTH), f32),
        "conv_ln_g": gain(ks[6], (D, CONV_WIDTH)),
        "conv_ln_b": 0.02 * jax.random.normal(ks[7], (D, CONV_WIDTH), f32),
        "w_conv_out": w(ks[8], (D, CONV_WIDTH, D_MODEL), CONV_WIDTH),
        "w_attn_out": w(ks[9], (D, ATTN_OUT_WIDTH, D_MODEL), ATTN_OUT_WIDTH),
        "w_o": w(ks[10], (D, D_MODEL, D_MODEL), D_MODEL),
        "norm2_g": gain(ks[11], (D, D_MODEL)),
        "w_ffn_gate": w(ks[12], (D, D_MODEL, D_FF), D_MODEL),
        "w_ffn_up": w(ks[13], (D, D_MODEL, D_FF), D_MODEL),
        "w_ffn_down": w(ks[14], (D, D_FF, D_MODEL), D_FF),
        "norm_f_g": gain(ks[15], (D_MODEL,)),
    }


def reference(x, norm1_g, w_in, gate_b, conv_w, conv_b, conv_ln_g, conv_ln_b,
              w_conv_out, w_attn_out, w_o, norm2_g, w_ffn_gate, w_ffn_up,
              w_ffn_down, norm_f_g):
    B, S, _ = x.shape
    slopes = jnp.asarray(_alibi_slopes())
    splits = [ATTN_WIDTH, 2 * ATTN_WIDTH, 3 * ATTN_WIDTH, 3 * ATTN_WIDTH + 2 * CONV_WIDTH]
    for l in range(DEPTH):
        h = _rmsnorm(x, norm1_g[l])
        proj = h @ w_in[l]
        q, k, v, u, g_logits = jnp.split(proj, splits, axis=-1)
        q = q.reshape(B, S, N_GROUPS, HEADS_PER_GROUP, HEAD_DIM)
        k = k.reshape(B, S, N_GROUPS, HEADS_PER_GROUP, HEAD_DIM)
        v = v.reshape(B, S, N_GROUPS, HEADS_PER_GROUP, HEAD_DIM)

        outs, lses = [], []
        for g, (window, dilation) in enumerate(GROUPS):
            o, lse = _dilated_group(q[:, :, g], k[:, :, g], v[:, :, g], slopes[g], window, dilation)
            outs.append(o)
            lses.append(lse)
        alpha = jax.nn.softmax(jnp.stack(lses, axis=0), axis=0)
        y_attn = jnp.sum(alpha[..., None] * jnp.stack(outs, axis=0), axis=0)
        y_attn = y_attn.reshape(B, S, ATTN_OUT_WIDTH).astype(x.dtype) @ w_attn_out[l]

        ua, ub = jnp.split(u, 2, axis=-1)
        c = ua * jax.nn.sigmoid(ub)
        c = _causal_depthwise_conv(c, conv_w[l], conv_b[l])
        c = jax.nn.silu(_layernorm(c, conv_ln_g[l], conv_ln_b[l]))
        y_conv = c @ w_conv_out[l]

        gates = jax.nn.sigmoid(g_logits + gate_b[l])
        g_attn, g_conv = jnp.split(gates, 2, axis=-1)
        x = x + (g_attn * y_attn + g_conv * y_conv) @ w_o[l]

        h2 = _rmsnorm(x, norm2_g[l])
        x = x + (jax.nn.silu(h2 @ w_ffn_gate[l]) * (h2 @ w_ffn_up[l])) @ w_ffn_down[l]
    return _rmsnorm(x, norm_f_g)
```

```python
import math
import os
from contextlib import ExitStack

import numpy as np
import ml_dtypes
import concourse.bass as bass
import concourse.mybir as mybir
from concourse.bass_utils import run_bass_kernel_spmd

F32 = mybir.dt.float32
BF16 = mybir.dt.bfloat16
AF = mybir.ActivationFunctionType
ALU = mybir.AluOpType

D = 1024
S = 2048
NSEQ = 2
NT = S // 128
DFF = 2816
NF = DFF // 128
KC = D // 128
IN_W = 8704
GROUPS = ((1, 2048), (4, 512), (16, 128))
CK = 31
RMS_EPS = 1e-6
LN_EPS = 1e-5
Q_OFF, K_OFF, V_OFF, U_OFF, G_OFF = 0, 1536, 3072, 4608, 6656


def _alibi_slope_list(n):
    def pow2(m):
        start = 2.0 ** (-8.0 / m)
        return [start ** (i + 1) for i in range(m)]
    if math.log2(n).is_integer():
        return pow2(n)
    c = 2 ** math.floor(math.log2(n))
    return pow2(c) + _alibi_slope_list(2 * c)[0::2][: n - c]


def _alibi_slopes():
    s = sorted(_alibi_slope_list(24), reverse=True)
    return np.asarray(s, dtype=np.float32).reshape(3, 8)


def _etable():
    sl = _alibi_slopes().astype(np.float64)
    k = np.arange(128)[:, None]
    q = np.arange(128)[None, :]
    E = np.zeros((128, 24, 2, 128), np.float32)
    for g, (r, _) in enumerate(GROUPS):
        for h in range(8):
            sr = float(np.float32(sl[g, h])) * r
            cur = np.where(q >= k, np.exp(-sr * (q - k)), 0.0)
            prev = np.where(k >= q, np.exp(-sr * (128 + q - k)), 0.0)
            E[:, g * 8 + h, 0, :] = cur
            E[:, g * 8 + h, 1, :] = prev
    return E.reshape(128, 24 * 2 * 128)


class Tok:
    __slots__ = ("key", "sem", "val", "eng", "epoch")

    def __init__(self, key, sem, val, eng, epoch):
        self.key, self.sem, self.val, self.eng, self.epoch = key, sem, val, eng, epoch


class Res:
    __slots__ = ("writers", "readers")

    def __init__(self):
        self.writers = {}
        self.readers = {}


class KB:
    NEP = 16
    NDS = 24

    def __init__(self, nc, es):
        self.nc = nc
        self.engs = {"pe": nc.tensor, "act": nc.scalar, "dve": nc.vector, "pool": nc.gpsimd, "sp": nc.sync}
        self.esem = {e: [es.enter_context(nc.semaphore(f"s_{e}_{i}")) for i in range(self.NEP)]
                     for e in ("pe", "act", "dve", "pool")}
        self.dsem = [es.enter_context(nc.semaphore(f"sd_{i}")) for i in range(self.NDS)]
        self.bsem = es.enter_context(nc.semaphore("s_bar"))
        self.nbar = 0
        self.epoch = 0
        self.cnt = {e: 0 for e in ("pe", "act", "dve", "pool")}
        self.last = {e: None for e in ("pe", "act", "dve", "pool")}
        self.dcnt = [0] * self.NDS
        self.dtok = [None] * self.NDS
        self.dnext = 0
        self.waited = {}

    def _wait(self, e, toks):
        best = {}
        for t in toks:
            if t is None or t.epoch < self.epoch:
                continue
            if t.eng == "pe" and e == "pe":
                continue
            if best.get(t.key) is None or best[t.key].val < t.val:
                best[t.key] = t
        for key, t in best.items():
            if self.waited.get((e, key), 0) >= t.val:
                continue
            self.engs[e].wait_ge(t.sem, t.val)
            self.waited[(e, key)] = t.val

    def _deps(self, reads, writes):
        toks = []
        for r in reads:
            toks += list(r.writers.values())
        for w in writes:
            toks += list(w.readers.values())
            toks += list(w.writers.values())
        return toks

    def _note(self, tok, reads, writes):
        for r in reads:
            old = r.readers.get(tok.key)
            if old is None or old.val < tok.val or old.epoch < tok.epoch:
                r.readers[tok.key] = tok
        for w in writes:
            if w.readers:
                w.readers = {}
                w.writers = {}
            w.writers[tok.key] = tok

    def emit(self, e, fn, reads=(), writes=(), inc=True):
        self._wait(e, self._deps(reads, writes))
        ins = fn(self.engs[e])
        sem = self.esem[e][self.epoch]
        if inc:
            self.cnt[e] += 1
            ins.then_inc(sem, 1)
            tok = Tok((e, self.epoch), sem, self.cnt[e], e, self.epoch)
            self.last[e] = tok
        else:
            assert e == "pe"
            tok = Tok((e, self.epoch), sem, self.cnt[e] + 1, e, self.epoch)
        self._note(tok, reads, writes)
        return tok

    def dma(self, q, out, in_, reads=(), writes=()):
        toks = []
        for r in reads:
            toks += list(r.writers.values())
        for w in writes:
            toks += list(w.readers.values())
            toks += [t for t in w.writers.values() if t.eng != "dma"]
        idx = self.dnext
        self.dnext = (idx + 1) % self.NDS
        toks.append(self.dtok[idx])
        self._wait(q, toks)
        ins = self.engs[q].dma_start(out=out, in_=in_)
        self.dcnt[idx] += 16
        ins.then_inc(self.dsem[idx], 16)
        tok = Tok(("d", idx), self.dsem[idx], self.dcnt[idx], "dma", self.epoch)
        self.dtok[idx] = tok
        self._note(tok, reads, writes)
        return tok

    def barrier(self):
        toks = [self.last[e] for e in ("pe", "act", "dve", "pool")] + list(self.dtok)
        self._wait("sp", toks)
        self.nbar += 1
        self.nc.sync.sem_inc(self.bsem, 1)
        for e in ("pe", "act", "dve", "pool"):
            self.engs[e].wait_ge(self.bsem, self.nbar)
        self.epoch += 1
        assert self.epoch < self.NEP
        for e in self.cnt:
            self.cnt[e] = 0
            self.last[e] = None
        self.waited = {}


def build_nc(stage=99):
    nc = bass.Bass("TRN2", target_bir_lowering=False)

    def din(name, shape):
        return nc.dram_tensor(name, list(shape), F32, kind="ExternalInput").ap()

    x_d = din("x", (NSEQ, S, D))
    w_in = din("w_in", (D, IN_W))
    w_co = din("w_conv_out", (D, D))
    w_ao = din("w_attn_out", (512, D))
    w_o = din("w_o", (D, D))
    w_fg = din("w_ffn_gate", (D, DFF))
    w_fu = din("w_ffn_up", (D, DFF))
    w_fd = din("w_ffn_down", (DFF, D))
    g1_d = din("g1", (1, D))
    g2_d = din("g2", (1, D))
    gf_d = din("gf", (1, D))
    gb_d = din("gate_b", (128, 16))
    cw_d = din("conv_w", (128, KC * CK))
    cvec_d = din("cvec", (128, 3 * KC))
    et_d = nc.dram_tensor("etab", [128, 24 * 2 * 128], BF16, kind="ExternalInput").ap()
    id_d = din("ident", (128, 128))
    idb_d = nc.dram_tensor("identb", [128, 128], BF16, kind="ExternalInput").ap()
    out_d = nc.dram_tensor("out", [NSEQ, S, D], F32, kind="ExternalOutput").ap()
    dbg_d = None
    if stage < 99:
        dbg_d = nc.dram_tensor("dbg", [128, 8192], F32, kind="ExternalOutput").ap()

    with ExitStack() as es:
        E = es.enter_context
        k = KB(nc, es)

        uniq = [0]

        def sb(st, name, shape, dt):
            uniq[0] += 1
            return st.enter_context(nc.sbuf_tensor(f"{name}_u{uniq[0]}", list(shape), dt))

        ident = sb(es, "ident", (128, 128), BF16)
        onesm = sb(es, "onesm", (128, 128), BF16)
        opad = sb(es, "opad", (128, 2, 128), BF16)
        gbc = sb(es, "gbc", (128, D), F32)
        gatb = sb(es, "gatb", (128, 16), F32)
        cw = sb(es, "cw", (128, KC, CK), F32)
        cvec = sb(es, "cvec", (128, 3, KC), F32)
        epsr = sb(es, "epsr", (128, 2), F32)
        identf = sb(es, "identf", (128, 128), F32)
        r_const = Res()
        r_gbc = Res()
        k.dma("sp", ident[:], idb_d, writes=[r_const])
        k.dma("sp", identf[:], id_d, writes=[r_const])
        k.dma("sp", gatb[:], gb_d, writes=[r_const])
        k.dma("sp", cw[:].rearrange("p c k -> p (c k)"), cw_d, writes=[r_const])
        k.dma("sp", cvec[:].rearrange("p c k -> p (c k)"), cvec_d, writes=[r_const])
        k.emit("dve", lambda e: e.memset(onesm[:], 1.0 / 1024.0), writes=[r_const])
        k.emit("dve", lambda e: e.memset(opad[:], 0.0), writes=[r_const])
        k.emit("dve", lambda e: e.memset(opad[:, 0, 0:64], 1.0), writes=[r_const])
        k.emit("dve", lambda e: e.memset(opad[:, 1, 64:128], 1.0), writes=[r_const])
        k.emit("dve", lambda e: e.memset(epsr[:, 0:1], RMS_EPS), writes=[r_const])
        k.emit("dve", lambda e: e.memset(epsr[:, 1:2], LN_EPS), writes=[r_const])

        NBK = 6
        banks = [E(nc.psum_tensor(f"pb{i}", [128, 512], F32)) for i in range(NBK)]
        bres = [Res() for _ in range(NBK)]
        psts = [E(nc.psum_tensor(f"pst{i}", [128, KC, 128], BF16)) for i in range(2)]
        r_psts = [Res(), Res()]
        pst_i = [0]
        bank_i = [0]
        bank_n = [NBK]

        def bank():
            i = bank_i[0] % bank_n[0]
            bank_i[0] = (i + 1) % bank_n[0]
            return banks[i], bres[i]

        def load_gain(g_d):
            k.dma("sp", gbc[:], g_d.partition_broadcast(128), writes=[r_gbc])

        dbg_off = [0]

        def dump(ap_sb, ncols, res_list, is_bf16=False):
            o = dbg_off[0]
            with ExitStack() as st:
                tmp = sb(st, f"dbgtmp{o}", (128, ncols), F32)
                r = Res()
                k.emit("dve", lambda e: e.tensor_copy(out=tmp[:], in_=ap_sb), reads=res_list, writes=[r])
                k.dma("sp", dbg_d[:, o:o + ncols], tmp[:], reads=[r])
                k.barrier()
            dbg_off[0] = o + ncols

        def rms_front(src_ap, r_src, scr):
            hb, r_hb, ss, r_ss, junk, r_junk = scr
            k.emit("act", lambda e: e.activation(out=junk[:], in_=src_ap, func=AF.Square, accum_out=ss[:, 0:1]),
                   reads=[r_src], writes=[r_junk, r_ss])
            k.emit("act", lambda e: e.activation(out=ss[:, 1:2], in_=ss[:, 0:1], func=AF.Sqrt, scale=1.0 / D,
                                                 bias=epsr[:, 0:1]), reads=[r_ss, r_const], writes=[r_ss])
            k.emit("dve", lambda e: e.reciprocal(out=ss[:, 2:3], in_=ss[:, 1:2]), reads=[r_ss], writes=[r_ss])
            k.emit("dve", lambda e: e.scalar_tensor_tensor(out=hb[:], in0=src_ap, scalar=ss[:, 2:3], in1=gbc[:],
                                                           op0=ALU.mult, op1=ALU.mult),
                   reads=[r_src, r_ss, r_gbc], writes=[r_hb])
            pst, r_pst = psts[pst_i[0] % 2], r_psts[pst_i[0] % 2]
            pst_i[0] += 1
            for kc in range(KC):
                k.emit("pe", lambda e, kc=kc: e.transpose(pst[:, kc, :], hb[:, kc * 128:(kc + 1) * 128], ident[:]),
                       reads=[r_hb, r_const], writes=[r_pst], inc=(kc == KC - 1))
            return pst, r_pst

        def rms_back(pp, dstT, r_dst, col0):
            pst, r_pst = pp
            k.emit("act", lambda e: e.activation(out=dstT[:, 0:4, col0:col0 + 128], in_=pst[:, 0:4, :], func=AF.Copy),
                   reads=[r_pst], writes=[r_dst])
            k.emit("dve", lambda e: e.tensor_copy(out=dstT[:, 4:8, col0:col0 + 128], in_=pst[:, 4:8, :]),
                   reads=[r_pst], writes=[r_dst])

        for sq in range(NSEQ):
            with ExitStack() as seq_st:
                hT = sb(seq_st, "hT", (128, KC, S), BF16)
                r_hT4 = [Res() for _ in range(4)]
                r_hT = None
                A2 = sb(seq_st, "A2", (128, 16, 1024), BF16)
                mT = A2[:].rearrange("p a b -> p (a b)").rearrange("p (k s) -> p k s", k=KC)
                r_mT = Res()
                B16 = sb(seq_st, "B16", (128, 8192), BF16)
                dgs = [B16[:, i * 4096:i * 4096 + CK * 128].rearrange("p (k j) -> p k j", k=CK) for i in range(2)]
                r_dgs = [Res(), Res()]
                Wo = B16[:].rearrange("p (k n) -> p k n", k=KC)
                r_Wo = Res()
                ssb = [sb(seq_st, f"ss{i}", (128, 4), F32) for i in range(2)]
                r_ss = [Res(), Res()]
                scopeA = ExitStack()
                YnT = sb(scopeA, "YnT", (128, 4, S), BF16)
                r_YnT = Res()
                gsl = [sb(scopeA, f"gsl{i}", (128, KC, 384), BF16) for i in range(2)]
                r_gsl = [Res(), Res()]
                wsl = [sb(scopeA, f"wsl{i}", (128, 4, 128), BF16) for i in range(2)]

                def load_u(cc):
                    sl, rs = gsl[cc % 2], r_gsl[cc % 2]
                    for part in range(2):
                        c0 = U_OFF + part * 1024 + cc * 128
                        k.dma("pool", sl[:, :, part * 128:(part + 1) * 128],
                              w_in[:, c0:c0 + 128].rearrange("(kc p) n -> p kc n", p=128), writes=[rs])
                YD = A2[:].rearrange("p a b -> p (a b)").bitcast(F32).rearrange("p (a pr s) -> p a pr s", a=2, pr=2)
                r_YD = Res()

                def normalise(quad):
                    k.emit("act", lambda e: e.activation(out=YD[:, 1], in_=YD[:, 1], func=AF.Ln), reads=[r_YD], writes=[r_YD])
                    k.emit("act", lambda e: e.activation(out=YD[:, 1], in_=YD[:, 1], func=AF.Exp, scale=-1.0),
                           reads=[r_YD], writes=[r_YD])
                    k.emit("dve", lambda e: e.tensor_tensor(out=YnT[:, 2 * quad, :], in0=YD[:, 0, 0, :],
                                                            in1=YD[:, 1, 0, :], op=ALU.mult),
                           reads=[r_YD], writes=[r_YnT])
                    k.emit("pool", lambda e: e.tensor_tensor(out=YnT[:, 2 * quad + 1, :], in0=YD[:, 0, 1, :],
                                                             in1=YD[:, 1, 1, :], op=ALU.mult),
                           reads=[r_YD], writes=[r_YnT])

                with ExitStack() as st:
                    etab = sb(st, "etab", (128, 24, 2, 128), BF16)
                    r_et = Res()
                    k.dma("sp", etab[:].rearrange("p h s q -> p (h s q)"), et_d, writes=[r_et])
                    Qpad = sb(st, "Qpad", (128, 2, 2, S), BF16)
                    r_Q = Res()
                    kT = sb(st, "kT", (128, 2, S), BF16)
                    r_kT = Res()
                    Vpad = sb(st, "Vpad", (128, NT, 4, 128), BF16)
                    r_V = Res()
                    slabs = [sb(st, f"slab{i}", (128, KC, 768), BF16) for i in range(2)]
                    r_slab = [Res(), Res()]
                    pes = [sb(st, f"pe{i}", (128, 512), BF16) for i in range(2)]
                    r_pes = [Res(), Res()]
                    pe2s = [sb(st, f"pe2{i}", (128, 512), BF16) for i in range(4)]
                    r_pe2s = [Res() for _ in range(4)]

                    def load_slab(j, g, quad):
                        sl, rs = slabs[j % 2], r_slab[j % 2]
                        for part, off in enumerate((Q_OFF, K_OFF, V_OFF)):
                            c0 = off + g * 512 + quad * 256
                            k.dma("pool", sl[:, :, part * 256:(part + 1) * 256],
                                  w_in[:, c0:c0 + 256].rearrange("(kc p) n -> p kc n", p=128), writes=[rs])

                    jobs = [(quad, g) for quad in range(2) for g in range(3)]
                    load_slab(0, jobs[0][1], jobs[0][0])
                    k.emit("pool", lambda e: e.memset(Qpad[:], 0.0), writes=[r_Q])
                    k.emit("pool", lambda e: e.memset(Vpad[:], 0.0), writes=[r_V])
                    pei_box = [0]

                    def proj_mm(sl, rs, r, tt):
                        res = []
                        for pair in range(2):
                            for which in range(2):
                                ps, rp = bank()
                                for kc in range(KC):
                                    k.emit("pe", lambda e, kc=kc, ps=ps, which=which, pair=pair: e.matmul(
                                        ps[:], lhsT=sl[:, kc, which * 256 + pair * 128: which * 256 + pair * 128 + 128],
                                        rhs=hT[:, kc, tt * 512:(tt + 1) * 512], start=(kc == 0), stop=(kc == KC - 1)),
                                        reads=[rs, r_hT4[tt]], writes=[rp], inc=(kc == KC - 1))
                                res.append((ps, rp, pair, which))
                        return res

                    def proj_ev(res, r, tt):
                        for ps, rp, pair, which in res:
                            n_i = 512 // r
                            off = n_i * tt

                            def views(p0, p1, dst_full, ps=ps, r=r, off=off, n_i=n_i):
                                if r == 1:
                                    return dst_full[p0:p1, off:off + 512], ps[p0:p1, :]
                                src = ps[p0:p1, :].rearrange("p (i c) -> p c i", c=r)
                                dst = dst_full[p0:p1, :].rearrange("p (c s) -> p c s", c=r)[:, :, off:off + n_i]
                                return dst, src
                            if which == 0:
                                d0, s0 = views(0, 64, Qpad[:, pair, 0, :])
                                d1, s1 = views(64, 128, Qpad[:, pair, 1, :])
                                k.emit("act", lambda e, d0=d0, s0=s0: e.activation(out=d0, in_=s0, func=AF.Copy),
                                       reads=[rp], writes=[r_Q])
                                k.emit("dve", lambda e, d1=d1, s1=s1: e.tensor_copy(out=d1, in_=s1),
                                       reads=[rp], writes=[r_Q])
                            else:
                                d0, s0 = views(0, 128, kT[:, pair, :])
                                k.emit("act", lambda e, d0=d0, s0=s0: e.activation(out=d0, in_=s0, func=AF.Copy),
                                       reads=[rp], writes=[r_kT])

                    def proj_qk(sl, rs, r, tt):
                        proj_ev(proj_mm(sl, rs, r, tt), r, tt)

                    b16f_p1 = B16[:].bitcast(F32)
                    xs = [b16f_p1[:, i * 1024:(i + 1) * 1024] for i in range(3)]
                    r_xs = [Res() for _ in range(3)]
                    hb = [B16[:, 6144:7168], B16[:, 7168:8192]]
                    r_hb = [Res(), Res()]
                    junk = gsl[0][:].rearrange("p a b -> p (a b)")[:, 0:D]
                    r_junk = Res()
                    load_gain(g1_d)
                    for i in range(min(2, NT)):
                        k.dma("sp", xs[i % 3][:], x_d[sq, i * 128:(i + 1) * 128, :], writes=[r_xs[i % 3]])
                    sl0, rs0 = slabs[0], r_slab[0]

                    def p1_group(gq):
                        prev = None
                        for i in range(4 * gq, 4 * gq + 4):
                            b = i % 2
                            if i + 2 < NT:
                                k.dma("sp", xs[(i + 2) % 3][:], x_d[sq, (i + 2) * 128:(i + 3) * 128, :], writes=[r_xs[(i + 2) % 3]])
                            pp = rms_front(xs[i % 3][:], r_xs[i % 3], (hb[b], r_hb[b], ssb[b], r_ss[b], junk, r_junk))
                            if prev is not None:
                                rms_back(prev[0], hT, r_hT4[gq], prev[1])
                            prev = (pp, i * 128)
                        rms_back(prev[0], hT, r_hT4[gq], prev[1])

                    p1_group(0)
                    pend = proj_mm(sl0, rs0, 1, 0)
                    for gq in range(1, 4):
                        p1_group(gq)
                        proj_ev(pend, 1, gq - 1)
                        pend = proj_mm(sl0, rs0, 1, gq)
                    proj_ev(pend, 1, 3)
                    if stage == 1:
                        k.barrier()
                        dump(hT[:, 0, :], S, r_hT4)
                        dump(hT[:, 7, :], S, r_hT4)
                        scopeA.close()
                        break

                    for j, (quad, g) in enumerate(jobs):
                        if j + 1 < len(jobs):
                            load_slab(j + 1, jobs[j + 1][1], jobs[j + 1][0])
                        sl, rs = slabs[j % 2], r_slab[j % 2]
                        r, L = GROUPS[g]
                        nb = L // 128
                        if j > 0:
                            for tt in range(4):
                                proj_qk(sl, rs, r, tt)
                        for b2 in range(NT // 2):
                            ps, rp = bank()
                            for bb in range(2):
                                blk = 2 * b2 + bb
                                c, n = blk // nb, blk % nb
                                t0 = r * 128 * n + c
                                for kc in range(KC):
                                    k.emit("pe", lambda e, kc=kc, ps=ps, bb=bb, t0=t0, r=r: e.matmul(
                                        ps[:, bb * 256:(bb + 1) * 256],
                                        lhsT=hT[:, kc, t0:t0 + 127 * r + 1:r],
                                        rhs=sl[:, kc, 512:768], start=(kc == 0), stop=(kc == KC - 1)),
                                        reads=[rs] + ([r_hT4[blk // 4]] if r == 1 else r_hT4), writes=[rp], inc=(kc == KC - 1))
                            for par in range(2):
                                src = ps[:].rearrange("p (b pr two d) -> p b pr two d", b=2, pr=2, two=2)[:, :, :, par, :]
                                dst = Vpad[:, 2 * b2:2 * b2 + 2, :, :].rearrange(
                                    "p b (pr two) d -> p b pr two d", two=2)[:, :, :, par, par * 64:(par + 1) * 64]
                                eng = "act" if par == 0 else "dve"
                                if eng == "act":
                                    k.emit("act", lambda e, dst=dst, src=src: e.activation(out=dst, in_=src, func=AF.Copy),
                                           reads=[rp], writes=[r_V])
                                else:
                                    k.emit("dve", lambda e, dst=dst, src=src: e.tensor_copy(out=dst, in_=src),
                                           reads=[rp], writes=[r_V])
                        gh0 = g * 8 + quad * 4

                        def s_stage(blk):
                            nonlocal_pei = pei_box
                            n = blk % nb
                            chunks = [(blk, 0)] + ([(blk - 1, 1)] if n > 0 else [])
                            pbs = []
                            for ci, (kb, sel) in enumerate(chunks):
                                ps, rp = bank()
                                for pr in range(2):
                                    k.emit("pe", lambda e, ps=ps, pr=pr, kb=kb, blk=blk: e.matmul(
                                        ps[:, pr * 256:(pr + 1) * 256].rearrange("p (a q) -> p a q", a=2),
                                        lhsT=kT[:, pr, kb * 128:(kb + 1) * 128],
                                        rhs=Qpad[:, pr, :, blk * 128:(blk + 1) * 128], start=True, stop=True),
                                        reads=[r_kT, r_Q], writes=[rp], inc=(pr == 1))
                                pi = nonlocal_pei[0]
                                nonlocal_pei[0] += 1
                                pb, rpb = pes[pi % 2], r_pes[pi % 2]
                                pb2, rpb2 = pe2s[pi % 4], r_pe2s[pi % 4]
                                k.emit("act", lambda e, pb=pb, ps=ps: e.activation(out=pb[:], in_=ps[:], func=AF.Exp, scale=0.125),
                                       reads=[rp], writes=[rpb])
                                k.emit("pool" if sel == 1 else "dve", lambda e, pb=pb, pb2=pb2, sel=sel: e.tensor_tensor(
                                    out=pb2[:].rearrange("p (h q) -> p h q", h=4), in0=pb[:].rearrange("p (h q) -> p h q", h=4),
                                    in1=etab[:, gh0:gh0 + 4, sel, :], op=ALU.mult),
                                    reads=[rpb, r_et], writes=[rpb2])
                                pbs.append((kb, pb2, rpb2))
                            return pbs

                        def pv_stage(blk, pbs):
                            c, n = blk // nb, blk % nb
                            pa, rpa = bank()
                            pav = pa[:].rearrange("p (a pr q) -> p a pr q", a=2, pr=2)
                            nmm = 2 * len(pbs)
                            for pr in range(2):
                                im = 0
                                for (kb, pb2, rpb2) in pbs:
                                    for par in range(2):
                                        h = 2 * pr + par
                                        k.emit("pe", lambda e, pr=pr, h=h, kb=kb, pb2=pb2, im=im, pav=pav, nmm=nmm: e.matmul(
                                            pav[:, 0, pr, :], lhsT=Vpad[:, kb, h, :], rhs=pb2[:, h * 128:(h + 1) * 128],
                                            start=(im == 0), stop=(im == nmm - 1)),
                                            reads=[r_V, rpb2], writes=[rpa], inc=(im == nmm - 1))
                                        im += 1
                            im = 0
                            for (kb, pb2, rpb2) in pbs:
                                pbv = pb2[:].rearrange("p (pr two q) -> p pr two q", pr=2, two=2)
                                for par in range(2):
                                    k.emit("pe", lambda e, par=par, pbv=pbv, im=im, pav=pav, nmm=nmm: e.matmul(
                                        pav[:, 1, :, :], lhsT=opad[:, par, :], rhs=pbv[:, :, par, :],
                                        start=(im == 0), stop=(im == nmm - 1)),
                                        reads=[r_const, rpb2], writes=[rpa], inc=(im == nmm - 1))
                                    im += 1
                            t0 = r * 128 * n + c
                            dst = YD[:, :, :, t0:t0 + 127 * r + 1:r]
                            if g == 0:
                                k.emit("dve", lambda e, dst=dst, pav=pav: e.tensor_copy(out=dst, in_=pav),
                                       reads=[rpa], writes=[r_YD])
                            else:
                                k.emit("dve", lambda e, dst=dst, pav=pav: e.tensor_tensor(out=dst, in0=pav, in1=dst, op=ALU.add),
                                       reads=[rpa, r_YD], writes=[r_YD])

                        nxt = s_stage(0)
                        for blk in range(NT):
                            cur = nxt
                            if blk + 1 < NT:
                                nxt = s_stage(blk + 1)
                            pv_stage(blk, cur)
                        if j == len(jobs) - 1:
                            load_u(0)
                            load_u(1)
                        if g == 2 and j < len(jobs) - 1:
                            normalise(quad)
                    k.barrier()
                normalise(1)
                if stage == 2:
                    for c4 in range(4):
                        dump(YnT[:, c4, :], S, [r_YnT])
                    scopeA.close()
                    break

                with scopeA as st3:
                    cT = sb(st3, "cT", (128, KC, S + 32), BF16)
                    r_cT = [Res() for _ in range(KC)]
                    lnT = cT[:, :, 0:S]
                    r_lnT = [Res() for _ in range(4)]
                    k.emit("pool", lambda e: e.memset(cT[:, :, 0:32], 0.0), writes=r_cT)
                    gg = [sb(st3, f"gg{i}", (128, 2, 512), F32) for i in range(2)]
                    r_gg = [Res(), Res()]
                    cvs = [sb(st3, f"cv{i}", (128, KC, 512), F32) for i in range(2)]
                    r_cvs = [[Res() for _ in range(KC)] for _ in range(2)]
                    cvb = [sb(st3, f"cvb{i}", (128, 512), BF16) for i in range(2)]
                    r_cvb = [Res(), Res()]
                    cvq = [sb(st3, f"cvq{i}", (128, 512), BF16) for i in range(2)]
                    r_cvq = [Res(), Res()]
                    stt = sb(st3, "stt", (128, 4, 512), F32)
                    r_st = Res()

                    def load_g(oc):
                        sl, rs = gsl[oc % 2], r_gsl[oc % 2]
                        for part in range(2):
                            c0 = G_OFF + part * 1024 + oc * 128
                            k.dma("pool", sl[:, :, part * 128:(part + 1) * 128],
                                  w_in[:, c0:c0 + 128].rearrange("(kc p) n -> p kc n", p=128), writes=[rs])
                        k.dma("pool", sl[:, :, 256:384],
                              w_co[:, oc * 128:(oc + 1) * 128].rearrange("(kc p) n -> p kc n", p=128), writes=[rs])
                        k.dma("pool", wsl[oc % 2][:],
                              w_ao[:, oc * 128:(oc + 1) * 128].rearrange("(kc p) n -> p kc n", p=128), writes=[rs])

                    si = 0
                    for cc in range(KC):
                        if 1 <= cc and cc + 1 < KC:
                            load_u(cc + 1)
                        sl, rs = gsl[cc % 2], r_gsl[cc % 2]
                        for tt in range(4):
                            pA, rA = bank()
                            pB, rB = bank()
                            for part, (pp, rr) in enumerate(((pA, rA), (pB, rB))):
                                for kc in range(KC):
                                    k.emit("pe", lambda e, kc=kc, pp=pp, part=part, sl=sl, tt=tt: e.matmul(
                                        pp[:], lhsT=sl[:, kc, part * 128:(part + 1) * 128],
                                        rhs=hT[:, kc, tt * 512:(tt + 1) * 512], start=(kc == 0), stop=(kc == KC - 1)),
                                        reads=[rs, r_hT4[tt]], writes=[rr], inc=(kc == KC - 1))
                            sg, rsg = gg[si % 2], r_gg[si % 2]
                            si += 1
                            k.emit("act", lambda e, sg=sg, pB=pB: e.activation(out=sg[:, 0, :], in_=pB[:], func=AF.Sigmoid),
                                   reads=[rB], writes=[rsg])
                            k.emit("dve", lambda e, sg=sg, pA=pA, cc=cc, tt=tt: e.tensor_tensor(
                                out=cT[:, cc, 32 + tt * 512:32 + (tt + 1) * 512], in0=pA[:], in1=sg[:, 0, :], op=ALU.mult),
                                reads=[rA, rsg], writes=[r_cT[cc]])
                    load_g(0)
                    load_g(1)
                    if stage == 3:
                        k.barrier()
                        dump(cT[:, 0, 32:32 + S], S, [r_cT[0]])
                        dump(cT[:, 7, 32:32 + S], S, [r_cT[7]])
                        break

                    built = set()

                    def build_dg(j):
                        if j in built or j >= 4 * KC:
                            return
                        built.add(j)
                        cc_ = j % KC
                        for eng, t0_, t1_ in (("dve", 0, 15), ("pool", 15, CK)):
                            nt_ = t1_ - t0_
                            k.emit(eng, lambda e, j=j, cc_=cc_, t0_=t0_, t1_=t1_, nt_=nt_: e.tensor_tensor(
                                out=dgs[j % 2][:, t0_:t1_, :], in0=identf[:].unsqueeze(1).to_broadcast([128, nt_, 128]),
                                in1=cw[:, cc_, t0_:t1_].unsqueeze(2).to_broadcast([128, nt_, 128]), op=ALU.mult),
                                reads=[r_const], writes=[r_dgs[j % 2]])
                    build_dg(0)
                    bank_n[0] = 4
                    pend_stats = []

                    def flush_stats():
                        while pend_stats:
                            b_, cc_ = pend_stats.pop(0)
                            k.emit("pe", lambda e, b_=b_, cc_=cc_: e.matmul(banks[4][:], lhsT=onesm[:], rhs=cvb[b_][:],
                                                                            start=(cc_ == 0), stop=(cc_ == KC - 1)),
                                   reads=[r_cvb[b_], r_const], writes=[bres[4]])
                            k.emit("pe", lambda e, b_=b_, cc_=cc_: e.matmul(banks[5][:], lhsT=onesm[:], rhs=cvq[b_][:],
                                                                            start=(cc_ == 0), stop=(cc_ == KC - 1)),
                                   reads=[r_cvq[b_], r_const], writes=[bres[5]])
                    def z_phase(tz, chunks=tuple(range(KC))):
                        cvz, r_cvz = cvs[tz % 2], r_cvs[tz % 2]
                        for cc in chunks:
                            zeng = "pool" if cc in (1, 5) else "dve"
                            k.emit(zeng, lambda e, cc=cc: e.tensor_tensor(out=cvz[:, cc, :], in0=cvz[:, cc, :], in1=stt[:, 2, :], op=ALU.mult),
                                   reads=[r_cvz[cc], r_st], writes=[r_cvz[cc]])
                            k.emit(zeng, lambda e, cc=cc: e.tensor_tensor(out=cvz[:, cc, :], in0=cvz[:, cc, :], in1=stt[:, 3, :], op=ALU.add),
                                   reads=[r_cvz[cc], r_st], writes=[r_cvz[cc]])
                            k.emit("act", lambda e, cc=cc: e.activation(out=lnT[:, cc, tz * 512:(tz + 1) * 512], in_=cvz[:, cc, :], func=AF.Silu,
                                                                        scale=cvec[:, 1, cc:cc + 1], bias=cvec[:, 2, cc:cc + 1]),
                                   reads=[r_cvz[cc], r_const], writes=[r_lnT[tz]])

                    for tt in range(4):
                        cv, r_cv = cvs[tt % 2], r_cvs[tt % 2]
                        pM1, rM1 = banks[4], bres[4]
                        pM2, rM2 = banks[5], bres[5]
                        for cc in range(KC):
                            jj = tt * KC + cc
                            dg, r_dg = dgs[jj % 2], r_dgs[jj % 2]
                            build_dg(jj + 1)
                            if tt > 0 and 2 <= cc <= 5:
                                z_phase(tt - 1, (2 * (cc - 2), 2 * (cc - 2) + 1))
                            ps, rp = bank()
                            for kk in range(CK):
                                c0 = 2 + tt * 512 + kk
                                k.emit("pe", lambda e, kk=kk, ps=ps, cc=cc, c0=c0, dg=dg: e.matmul(
                                    ps[:], lhsT=dg[:, kk, :], rhs=cT[:, cc, c0:c0 + 512], start=(kk == 0), stop=(kk == CK - 1)),
                                    reads=[r_dg, r_cT[cc]], writes=[rp], inc=(kk == CK - 1))
                            flush_stats()
                            k.emit("act", lambda e, ps=ps, cc=cc: e.activation(out=cv[:, cc, :], in_=ps[:], func=AF.Identity,
                                                                               bias=cvec[:, 0, cc:cc + 1]),
                                   reads=[rp, r_const], writes=[r_cv[cc]])
                            b = cc % 2
                            k.emit("act", lambda e, cc=cc, b=b: e.activation(out=cvb[b][:], in_=cv[:, cc, :], func=AF.Copy),
                                   reads=[r_cv[cc]], writes=[r_cvb[b]])
                            k.emit("act", lambda e, cc=cc, b=b: e.activation(out=cvq[b][:], in_=cv[:, cc, :], func=AF.Square),
                                   reads=[r_cv[cc]], writes=[r_cvq[b]])
                            pend_stats.append((b, cc))
                        build_dg(tt * KC + KC + 1)
                        flush_stats()
                        k.emit("act", lambda e, pM1=pM1: e.activation(out=stt[:, 0, :], in_=pM1[:], func=AF.Copy),
                               reads=[rM1], writes=[r_st])
                        k.emit("dve", lambda e: e.tensor_tensor(out=stt[:, 1, :], in0=stt[:, 0, :], in1=stt[:, 0, :], op=ALU.mult),
                               reads=[r_st], writes=[r_st])
                        k.emit("dve", lambda e, pM2=pM2: e.tensor_tensor(out=stt[:, 1, :], in0=pM2[:], in1=stt[:, 1, :], op=ALU.subtract),
                               reads=[rM2, r_st], writes=[r_st])
                        k.emit("act", lambda e: e.activation(out=stt[:, 1, :], in_=stt[:, 1, :], func=AF.Sqrt, bias=epsr[:, 1:2]),
                               reads=[r_st, r_const], writes=[r_st])
                        k.emit("dve", lambda e: e.reciprocal(out=stt[:, 2, :], in_=stt[:, 1, :]), reads=[r_st], writes=[r_st])
                        k.emit("dve", lambda e: e.scalar_tensor_tensor(out=stt[:, 3, :], in0=stt[:, 0, :], scalar=-1.0, in1=stt[:, 2, :],
                                                                       op0=ALU.mult, op1=ALU.mult), reads=[r_st], writes=[r_st])
                    bank_n[0] = NBK

                    ggi = 0
                    for oc in range(KC):
                        if 1 <= oc and oc + 1 < KC:
                            load_g(oc + 1)
                        k.dma("pool", Wo[:, oc, :], w_o[oc * 128:(oc + 1) * 128, :], writes=[r_Wo, r_dgs[0], r_dgs[1]])
                        sl, rs, ws = gsl[oc % 2], r_gsl[oc % 2], wsl[oc % 2]
                        for tt in range(4):
                            if oc == 0 and tt == 2:
                                z_phase(3)
                            p1, rp1 = bank()
                            p3, rp3 = bank()
                            p4, rp4 = bank()
                            p2, rp2 = bank()
                            for kc in range(4):
                                k.emit("pe", lambda e, kc=kc, p1=p1, ws=ws, tt=tt: e.matmul(
                                    p1[:], lhsT=ws[:, kc, :], rhs=YnT[:, kc, tt * 512:(tt + 1) * 512],
                                    start=(kc == 0), stop=(kc == 3)), reads=[rs, r_YnT], writes=[rp1], inc=(kc == 3))
                            for part, (pp, rr) in enumerate(((p3, rp3), (p4, rp4))):
                                for kc in range(KC):
                                    k.emit("pe", lambda e, kc=kc, pp=pp, part=part, sl=sl, tt=tt: e.matmul(
                                        pp[:], lhsT=sl[:, kc, part * 128:(part + 1) * 128], rhs=hT[:, kc, tt * 512:(tt + 1) * 512],
                                        start=(kc == 0), stop=(kc == KC - 1)), reads=[rs, r_hT4[tt]], writes=[rr], inc=(kc == KC - 1))
                            for kc in range(KC):
                                k.emit("pe", lambda e, kc=kc, p2=p2, sl=sl, tt=tt: e.matmul(
                                    p2[:], lhsT=sl[:, kc, 256:384], rhs=lnT[:, kc, tt * 512:(tt + 1) * 512],
                                    start=(kc == 0), stop=(kc == KC - 1)), reads=[rs, r_lnT[tt]], writes=[rp2], inc=(kc == KC - 1))
                            g2, rg2 = gg[ggi % 2], r_gg[ggi % 2]
                            ggi += 1
                            k.emit("act", lambda e, g2=g2, p3=p3, oc=oc: e.activation(out=g2[:, 0, :], in_=p3[:], func=AF.Sigmoid,
                                                                                      bias=gatb[:, oc:oc + 1]),
                                   reads=[rp3, r_const], writes=[rg2])
                            k.emit("act", lambda e, g2=g2, p4=p4, oc=oc: e.activation(out=g2[:, 1, :], in_=p4[:], func=AF.Sigmoid,
                                                                                      bias=gatb[:, 8 + oc:9 + oc]),
                                   reads=[rp4, r_const], writes=[rg2])
                            k.emit("dve", lambda e, g2=g2, p1=p1: e.tensor_tensor(out=g2[:, 0, :], in0=p1[:], in1=g2[:, 0, :], op=ALU.mult),
                                   reads=[rp1, rg2], writes=[rg2])
                            k.emit("dve", lambda e, g2=g2, p2=p2: e.tensor_tensor(out=g2[:, 1, :], in0=p2[:], in1=g2[:, 1, :], op=ALU.mult),
                                   reads=[rp2, rg2], writes=[rg2])
                            k.emit("dve", lambda e, g2=g2, oc=oc, tt=tt: e.tensor_tensor(out=mT[:, oc, tt * 512:(tt + 1) * 512], in0=g2[:, 0, :],
                                                                                         in1=g2[:, 1, :], op=ALU.add),
                                   reads=[rg2], writes=[r_mT])
                    k.barrier()
                if stage == 4:
                    dump(mT[:, 0, :], S, [r_mT])
                    dump(mT[:, 7, :], S, [r_mT])
                    break

                with ExitStack() as scopeB:
                    x1 = sb(scopeB, "x1", (128, NT, D), F32)
                    r_x1 = [Res() for _ in range(NT)]
                    h2T, r_h2T = hT, Res()
                    NSLOT = 30
                    NPRE = NSLOT - NF
                    Wd = sb(scopeB, "Wd", (128, NSLOT, 512), BF16)
                    r_Wd = [Res() for _ in range(NSLOT)]

                    def wslot(h, f):
                        return (NF * h + f) % NSLOT
                    fsl = [sb(scopeB, f"fsl{i}", (128, KC, 256), BF16) for i in range(3)]
                    r_fsl = [Res(), Res(), Res()]

                    def load_wd(h, f_lo, f_hi):
                        half = h % 2
                        for f in range(f_lo, f_hi):
                            k.dma("pool", Wd[:, wslot(h, f), :], w_fd[f * 128:(f + 1) * 128, half * 512:(half + 1) * 512],
                                  writes=[r_Wd[wslot(h, f)]])

                    def load_f(jj, f):
                        sl, rs = fsl[jj % 3], r_fsl[jj % 3]
                        for part, wsrc in enumerate((w_fg, w_fu)):
                            k.dma("pool", sl[:, :, part * 128:(part + 1) * 128],
                                  wsrc[:, f * 128:(f + 1) * 128].rearrange("(kc p) n -> p kc n", p=128), writes=[rs])
                    with ExitStack() as st:
                        xs = [sb(st, f"xsb{i}", (128, D), F32) for i in range(2)]
                        r_xs = [Res(), Res()]
                        hb = [sb(st, f"hbb{i}", (128, D), BF16) for i in range(2)]
                        r_hb = [Res(), Res()]
                        ssb = [sb(st, f"ssb{i}", (128, 4), F32) for i in range(2)]
                        r_ss = [Res(), Res()]
                        junk = sb(st, "junkb", (128, D), BF16)
                        r_junk = Res()
                        load_gain(g2_d)
                        prev = None

                        def wo_mm(i):
                            res = []
                            for half in range(2):
                                ps, rp = bank()
                                for kc in range(KC):
                                    k.emit("pe", lambda e, kc=kc, ps=ps, half=half, i=i: e.matmul(
                                        ps[:], lhsT=mT[:, kc, i * 128:(i + 1) * 128], rhs=Wo[:, kc, half * 512:(half + 1) * 512],
                                        start=(kc == 0), stop=(kc == KC - 1)), reads=[r_mT, r_Wo], writes=[rp], inc=(kc == KC - 1))
                                res.append((ps, rp))
                            return res
                        k.dma("sp", xs[0][:], x_d[sq, 0:128, :], writes=[r_xs[0]])
                        nxt = wo_mm(0)
                        for i in range(NT):
                            b = i % 2
                            cur = nxt
                            if i == 3:
                                load_f(0, 0)
                                load_f(1, 1)
                                load_wd(0, 0, NF)
                            if i + 1 < NT:
                                k.dma("sp", xs[(i + 1) % 2][:], x_d[sq, (i + 1) * 128:(i + 2) * 128, :], writes=[r_xs[(i + 1) % 2]])
                                nxt = wo_mm(i + 1)
                            for half in range(2):
                                ps, rp = cur[half]
                                k.emit("dve", lambda e, ps=ps, half=half, i=i, b=b: e.tensor_tensor(
                                    out=x1[:, i, half * 512:(half + 1) * 512], in0=ps[:], in1=xs[b][:, half * 512:(half + 1) * 512], op=ALU.add),
                                    reads=[rp, r_xs[b]], writes=[r_x1[i]])
                            pp = rms_front(x1[:, i, :], r_x1[i], (hb[b], r_hb[b], ssb[b], r_ss[b], junk, r_junk))
                            if prev is not None:
                                rms_back(prev[0], h2T, r_h2T, prev[1])
                            prev = (pp, i * 128)
                        rms_back(prev[0], h2T, r_h2T, prev[1])
                        k.barrier()
                    if stage == 5:
                        dump(x1[:, 0, :], D, [r_x1[0]])
                        dump(x1[:, 15, :], D, [r_x1[15]])
                        dump(h2T[:, 0, :], S, [r_h2T])
                        break

                    with ExitStack() as st:
                        ST = 1024
                        aTb = sb(st, "aTb", (128, NF - 16, ST), BF16)
                        r_aT = Res()

                        def aT(f):
                            return A2[:, f, :] if f < 16 else aTb[:, f - 16, :]
                        b16f = B16[:].bitcast(F32)
                        sgs = [b16f[:, 2048 + i * 512:2048 + (i + 1) * 512] for i in range(2)]
                        r_sg = [Res(), Res()]
                        ssb = [sb(st, f"fss{i}", (128, 4), F32) for i in range(2)]
                        r_ss = [Res(), Res()]
                        junk = B16[:, 6144:7168]
                        r_junk = Res()
                        xo = [b16f[:, i * 1024:(i + 1) * 1024] for i in range(2)]
                        r_xo = [Res(), Res()]
                        load_gain(gf_d)

                        fj = 0
                        si = 0
                        for sti in range(S // ST):
                            for f in range(NF):
                                if f + 2 < NF:
                                    load_f(fj + 2, f + 2)
                                sl, rs = fsl[fj % 3], r_fsl[fj % 3]
                                fj += 1
                                for half in range(ST // 512):
                                    t0 = sti * ST + half * 512
                                    pG, rG = bank()
                                    pU, rU = bank()
                                    for part, (pp, rr) in enumerate(((pG, rG), (pU, rU))):
                                        for kc in range(KC):
                                            k.emit("pe", lambda e, kc=kc, pp=pp, part=part, sl=sl, t0=t0: e.matmul(
                                                pp[:], lhsT=sl[:, kc, part * 128:(part + 1) * 128], rhs=h2T[:, kc, t0:t0 + 512],
                                                start=(kc == 0), stop=(kc == KC - 1)), reads=[rs, r_h2T], writes=[rr], inc=(kc == KC - 1))
                                    sg, rsg = sgs[si % 2], r_sg[si % 2]
                                    si += 1
                                    k.emit("act", lambda e, sg=sg, pG=pG: e.activation(out=sg[:], in_=pG[:], func=AF.Silu),
                                           reads=[rG], writes=[rsg])
                                    k.emit("dve", lambda e, sg=sg, pU=pU, f=f, half=half: e.tensor_tensor(
                                        out=aT(f)[:, half * 512:(half + 1) * 512], in0=pU[:], in1=sg[:], op=ALU.mult),
                                        reads=[rU, rsg], writes=[r_aT])
                            if sti + 1 < S // ST:
                                load_f(fj, 0)
                                load_f(fj + 1, 1)
                            for half in range(2):
                                hg = sti * 2 + half
                                nh = 2 * (S // ST)
                                if hg + 1 < nh:
                                    load_wd(hg + 1, 0, NPRE)
                                for il in range(ST // 128):
                                    i = sti * (ST // 128) + il
                                    b = i % 2
                                    ps, rp = bank()
                                    last_tile = (il == ST // 128 - 1)
                                    for f in range(NF):
                                        k.emit("pe", lambda e, f=f, ps=ps, il=il, hg=hg: e.matmul(
                                            ps[:], lhsT=aT(f)[:, il * 128:(il + 1) * 128], rhs=Wd[:, wslot(hg, f), :],
                                            start=(f == 0), stop=(f == NF - 1)), reads=[r_aT, r_Wd[wslot(hg, f)]], writes=[rp],
                                            inc=(f == NF - 1 or last_tile))
                                    if last_tile and hg + 1 < nh:
                                        load_wd(hg + 1, NPRE, NF)
                                    k.emit("dve", lambda e, ps=ps, half=half, i=i: e.tensor_tensor(
                                        out=x1[:, i, half * 512:(half + 1) * 512], in0=ps[:], in1=x1[:, i, half * 512:(half + 1) * 512], op=ALU.add),
                                        reads=[rp, r_x1[i]], writes=[r_x1[i]])
                                    if half == 0:
                                        continue
                                    ss, rss = ssb[b], r_ss[b]
                                    k.emit("act", lambda e, i=i, ss=ss: e.activation(out=junk[:], in_=x1[:, i, :], func=AF.Square, accum_out=ss[:, 0:1]),
                                           reads=[r_x1[i]], writes=[r_junk, rss])
                                    k.emit("act", lambda e, ss=ss: e.activation(out=ss[:, 1:2], in_=ss[:, 0:1], func=AF.Sqrt, scale=1.0 / D,
                                                                                bias=epsr[:, 0:1]), reads=[rss, r_const], writes=[rss])
                                    k.emit("dve", lambda e, ss=ss: e.reciprocal(out=ss[:, 2:3], in_=ss[:, 1:2]), reads=[rss], writes=[rss])
                                    k.emit("dve", lambda e, ss=ss, i=i, b=b: e.scalar_tensor_tensor(
                                        out=xo[b][:], in0=x1[:, i, :], scalar=ss[:, 2:3], in1=gbc[:], op0=ALU.mult, op1=ALU.mult),
                                        reads=[r_x1[i], rss, r_gbc], writes=[r_xo[b]])
                                    k.dma("sp", out_d[sq, i * 128:(i + 1) * 128, :], xo[b][:], reads=[r_xo[b]])
                        k.barrier()
        k.barrier()
    return nc


def _host_inputs(x, norm1_g, w_in, gate_b, conv_w, conv_b, conv_ln_g, conv_ln_b, w_conv_out, w_attn_out, w_o,
                 norm2_g, w_ffn_gate, w_ffn_up, w_ffn_down, norm_f_g):
    f = lambda a: np.ascontiguousarray(np.asarray(a, dtype=np.float32))
    common = {
        "w_in": f(w_in[0]), "w_conv_out": f(w_conv_out[0]), "w_attn_out": f(w_attn_out[0]), "w_o": f(w_o[0]),
        "w_ffn_gate": f(w_ffn_gate[0]), "w_ffn_up": f(w_ffn_up[0]), "w_ffn_down": f(w_ffn_down[0]),
        "g1": f(norm1_g[0]).reshape(1, D), "g2": f(norm2_g[0]).reshape(1, D), "gf": f(norm_f_g).reshape(1, D),
        "gate_b": f(np.asarray(gate_b[0]).reshape(16, 128).T),
        "conv_w": f(np.asarray(conv_w[0]).reshape(CK, KC, 128).transpose(2, 1, 0).reshape(128, KC * CK)),
        "cvec": f(np.stack([np.asarray(v[0]).reshape(KC, 128).T for v in (conv_b, conv_ln_g, conv_ln_b)], axis=1).reshape(128, 3 * KC)),
        "etab": _etable().astype(ml_dtypes.bfloat16),
        "ident": np.eye(128, dtype=np.float32),
        "identb": np.eye(128, dtype=np.float32).astype(ml_dtypes.bfloat16),
    }
    xx = f(x)
    return [dict(common, x=np.ascontiguousarray(xx[2 * c:2 * c + 2])) for c in range(8)]


def kernel(**inputs):
    in_maps = _host_inputs(**inputs)
    nc = build_nc()
    res = run_bass_kernel_spmd(nc, in_maps, core_ids=list(range(8)))
    return np.concatenate([r["out"] for r in res.results], axis=0).astype(np.float32)
```

```python
import math
import os
from contextlib import ExitStack

import numpy as np
import ml_dtypes
import concourse.bass as bass
import concourse.mybir as mybir
from concourse.bass_utils import run_bass_kernel_spmd

F32 = mybir.dt.float32
BF16 = mybir.dt.bfloat16
AF = mybir.ActivationFunctionType
ALU = mybir.AluOpType

D = 1024
S = 2048
NSEQ = 2
NT = S // 128
DFF = 2816
NF = DFF // 128
KC = D // 128
IN_W = 8704
GROUPS = ((1, 2048), (4, 512), (16, 128))
CK = 31
RMS_EPS = 1e-6
LN_EPS = 1e-5
Q_OFF, K_OFF, V_OFF, U_OFF, G_OFF = 0, 1536, 3072, 4608, 6656


def _alibi_slope_list(n):
    def pow2(m):
        start = 2.0 ** (-8.0 / m)
        return [start ** (i + 1) for i in range(m)]
    if math.log2(n).is_integer():
        return pow2(n)
    c = 2 ** math.floor(math.log2(n))
    return pow2(c) + _alibi_slope_list(2 * c)[0::2][: n - c]


def _alibi_slopes():
    s = sorted(_alibi_slope_list(24), reverse=True)
    return np.asarray(s, dtype=np.float32).reshape(3, 8)


def _etable():
    sl = _alibi_slopes().astype(np.float64)
    k = np.arange(128)[:, None]
    q = np.arange(128)[None, :]
    E = np.zeros((128, 24, 2, 128), np.float32)
    for g, (r, _) in enumerate(GROUPS):
        for h in range(8):
            sr = float(np.float32(sl[g, h])) * r
            cur = np.where(q >= k, np.exp(-sr * (q - k)), 0.0)
            prev = np.where(k >= q, np.exp(-sr * (128 + q - k)), 0.0)
            E[:, g * 8 + h, 0, :] = cur
            E[:, g * 8 + h, 1, :] = prev
    return E.reshape(128, 24 * 2 * 128)


class Tok:
    __slots__ = ("key", "sem", "val", "eng", "epoch")

    def __init__(self, key, sem, val, eng, epoch):
        self.key, self.sem, self.val, self.eng, self.epoch = key, sem, val, eng, epoch


class Res:
    __slots__ = ("writers", "readers")

    def __init__(self):
        self.writers = {}
        self.readers = {}


class KB:
    NEP = 16
    NDS = 24

    def __init__(self, nc, es):
        self.nc = nc
        self.engs = {"pe": nc.tensor, "act": nc.scalar, "dve": nc.vector, "pool": nc.gpsimd, "sp": nc.sync}
        self.esem = {e: [es.enter_context(nc.semaphore(f"s_{e}_{i}")) for i in range(self.NEP)]
                     for e in ("pe", "act", "dve", "pool")}
        self.dsem = [es.enter_context(nc.semaphore(f"sd_{i}")) for i in range(self.NDS)]
        self.bsem = es.enter_context(nc.semaphore("s_bar"))
        self.nbar = 0
        self.epoch = 0
        self.cnt = {e: 0 for e in ("pe", "act", "dve", "pool")}
        self.last = {e: None for e in ("pe", "act", "dve", "pool")}
        self.dcnt = [0] * self.NDS
        self.dtok = [None] * self.NDS
        self.dnext = 0
        self.waited = {}

    def _wait(self, e, toks):
        best = {}
        for t in toks:
            if t is None or t.epoch < self.epoch:
                continue
            if t.eng == "pe" and e == "pe":
                continue
            if best.get(t.key) is None or best[t.key].val < t.val:
                best[t.key] = t
        for key, t in best.items():
            if self.waited.get((e, key), 0) >= t.val:
                continue
            self.engs[e].wait_ge(t.sem, t.val)
            self.waited[(e, key)] = t.val

    def _deps(self, reads, writes):
        toks = []
        for r in reads:
            toks += list(r.writers.values())
        for w in writes:
            toks += list(w.readers.values())
            toks += list(w.writers.values())
        return toks

    def _note(self, tok, reads, writes):
        for r in reads:
            old = r.readers.get(tok.key)
            if old is None or old.val < tok.val or old.epoch < tok.epoch:
                r.readers[tok.key] = tok
        for w in writes:
            if w.readers:
                w.readers = {}
                w.writers = {}
            w.writers[tok.key] = tok

    def emit(self, e, fn, reads=(), writes=(), inc=True):
        self._wait(e, self._deps(reads, writes))
        ins = fn(self.engs[e])
        sem = self.esem[e][self.epoch]
        if inc:
            self.cnt[e] += 1
            ins.then_inc(sem, 1)
            tok = Tok((e, self.epoch), sem, self.cnt[e], e, self.epoch)
            self.last[e] = tok
        else:
            assert e == "pe"
            tok = Tok((e, self.epoch), sem, self.cnt[e] + 1, e, self.epoch)
        self._note(tok, reads, writes)
        return tok

    def dma(self, q, out, in_, reads=(), writes=()):
        toks = []
        for r in reads:
            toks += list(r.writers.values())
        for w in writes:
            toks += list(w.readers.values())
            toks += [t for t in w.writers.values() if t.eng != "dma"]
        idx = self.dnext
        self.dnext = (idx + 1) % self.NDS
        toks.append(self.dtok[idx])
        self._wait(q, toks)
        ins = self.engs[q].dma_start(out=out, in_=in_)
        self.dcnt[idx] += 16
        ins.then_inc(self.dsem[idx], 16)
        tok = Tok(("d", idx), self.dsem[idx], self.dcnt[idx], "dma", self.epoch)
        self.dtok[idx] = tok
        self._note(tok, reads, writes)
        return tok

    def barrier(self):
        toks = [self.last[e] for e in ("pe", "act", "dve", "pool")] + list(self.dtok)
        self._wait("sp", toks)
        self.nbar += 1
        self.nc.sync.sem_inc(self.bsem, 1)
        for e in ("pe", "act", "dve", "pool"):
            self.engs[e].wait_ge(self.bsem, self.nbar)
        self.epoch += 1
        assert self.epoch < self.NEP
        for e in self.cnt:
            self.cnt[e] = 0
            self.last[e] = None
        self.waited = {}


def build_nc(stage=99):
    nc = bass.Bass("TRN2", target_bir_lowering=False)

    def din(name, shape):
        return nc.dram_tensor(name, list(shape), F32, kind="ExternalInput").ap()

    x_d = din("x", (NSEQ, S, D))
    w_in = din("w_in", (D, IN_W))
    w_co = din("w_conv_out", (D, D))
    w_ao = din("w_attn_out", (512, D))
    w_o = din("w_o", (D, D))
    w_fg = din("w_ffn_gate", (D, DFF))
    w_fu = din("w_ffn_up", (D, DFF))
    w_fd = din("w_ffn_down", (DFF, D))
    g1_d = din("g1", (1, D))
    g2_d = din("g2", (1, D))
    gf_d = din("gf", (1, D))
    gb_d = din("gate_b", (128, 16))
    cw_d = din("conv_w", (128, KC * CK))
    cvec_d = din("cvec", (128, 3 * KC))
    et_d = nc.dram_tensor("etab", [128, 24 * 2 * 128], BF16, kind="ExternalInput").ap()
    id_d = din("ident", (128, 128))
    idb_d = nc.dram_tensor("identb", [128, 128], BF16, kind="ExternalInput").ap()
    out_d = nc.dram_tensor("out", [NSEQ, S, D], F32, kind="ExternalOutput").ap()
    dbg_d = None
    if stage < 99:
        dbg_d = nc.dram_tensor("dbg", [128, 8192], F32, kind="ExternalOutput").ap()

    with ExitStack() as es:
        E = es.enter_context
        k = KB(nc, es)

        uniq = [0]

        def sb(st, name, shape, dt):
            uniq[0] += 1
            return st.enter_context(nc.sbuf_tensor(f"{name}_u{uniq[0]}", list(shape), dt))

        ident = sb(es, "ident", (128, 128), BF16)
        onesm = sb(es, "onesm", (128, 128), BF16)
        opad = sb(es, "opad", (128, 2, 128), BF16)
        gbc = sb(es, "gbc", (128, D), F32)
        gatb = sb(es, "gatb", (128, 16), F32)
        cw = sb(es, "cw", (128, KC, CK), F32)
        cvec = sb(es, "cvec", (128, 3, KC), F32)
        epsr = sb(es, "epsr", (128, 2), F32)
        identf = sb(es, "identf", (128, 128), F32)
        r_const = Res()
        r_gbc = Res()
        k.dma("sp", ident[:], idb_d, writes=[r_const])
        k.dma("sp", identf[:], id_d, writes=[r_const])
        k.dma("sp", gatb[:], gb_d, writes=[r_const])
        k.dma("sp", cw[:].rearrange("p c k -> p (c k)"), cw_d, writes=[r_const])
        k.dma("sp", cvec[:].rearrange("p c k -> p (c k)"), cvec_d, writes=[r_const])
        k.emit("dve", lambda e: e.memset(onesm[:], 1.0 / 1024.0), writes=[r_const])
        k.emit("dve", lambda e: e.memset(opad[:], 0.0), writes=[r_const])
        k.emit("dve", lambda e: e.memset(opad[:, 0, 0:64], 1.0), writes=[r_const])
        k.emit("dve", lambda e: e.memset(opad[:, 1, 64:128], 1.0), writes=[r_const])
        k.emit("dve", lambda e: e.memset(epsr[:, 0:1], RMS_EPS), writes=[r_const])
        k.emit("dve", lambda e: e.memset(epsr[:, 1:2], LN_EPS), writes=[r_const])

        NBK = 6
        banks = [E(nc.psum_tensor(f"pb{i}", [128, 512], F32)) for i in range(NBK)]
        bres = [Res() for _ in range(NBK)]
        psts = [E(nc.psum_tensor(f"pst{i}", [128, KC, 128], BF16)) for i in range(2)]
        r_psts = [Res(), Res()]
        pst_i = [0]
        bank_i = [0]
        bank_n = [NBK]

        def bank():
            i = bank_i[0] % bank_n[0]
            bank_i[0] = (i + 1) % bank_n[0]
            return banks[i], bres[i]

        def load_gain(g_d):
            k.dma("sp", gbc[:], g_d.partition_broadcast(128), writes=[r_gbc])

        dbg_off = [0]

        def dump(ap_sb, ncols, res_list, is_bf16=False):
            o = dbg_off[0]
            with ExitStack() as st:
                tmp = sb(st, f"dbgtmp{o}", (128, ncols), F32)
                r = Res()
                k.emit("dve", lambda e: e.tensor_copy(out=tmp[:], in_=ap_sb), reads=res_list, writes=[r])
                k.dma("sp", dbg_d[:, o:o + ncols], tmp[:], reads=[r])
                k.barrier()
            dbg_off[0] = o + ncols

        def rms_front(src_ap, r_src, scr):
            hb, r_hb, ss, r_ss, junk, r_junk = scr
            k.emit("act", lambda e: e.activation(out=junk[:], in_=src_ap, func=AF.Square, accum_out=ss[:, 0:1]),
                   reads=[r_src], writes=[r_junk, r_ss])
            k.emit("act", lambda e: e.activation(out=ss[:, 1:2], in_=ss[:, 0:1], func=AF.Sqrt, scale=1.0 / D,
                                                 bias=epsr[:, 0:1]), reads=[r_ss, r_const], writes=[r_ss])
            k.emit("dve", lambda e: e.reciprocal(out=ss[:, 2:3], in_=ss[:, 1:2]), reads=[r_ss], writes=[r_ss])
            k.emit("dve", lambda e: e.scalar_tensor_tensor(out=hb[:], in0=src_ap, scalar=ss[:, 2:3], in1=gbc[:],
                                                           op0=ALU.mult, op1=ALU.mult),
                   reads=[r_src, r_ss, r_gbc], writes=[r_hb])
            pst, r_pst = psts[pst_i[0] % 2], r_psts[pst_i[0] % 2]
            pst_i[0] += 1
            for kc in range(KC):
                k.emit("pe", lambda e, kc=kc: e.transpose(pst[:, kc, :], hb[:, kc * 128:(kc + 1) * 128], ident[:]),
                       reads=[r_hb, r_const], writes=[r_pst], inc=(kc == KC - 1))
            return pst, r_pst

        def rms_back(pp, dstT, r_dst, col0):
            pst, r_pst = pp
            k.emit("act", lambda e: e.activation(out=dstT[:, 0:4, col0:col0 + 128], in_=pst[:, 0:4, :], func=AF.Copy),
                   reads=[r_pst], writes=[r_dst])
            k.emit("dve", lambda e: e.tensor_copy(out=dstT[:, 4:8, col0:col0 + 128], in_=pst[:, 4:8, :]),
                   reads=[r_pst], writes=[r_dst])

        for sq in range(NSEQ):
            with ExitStack() as seq_st:
                hT = sb(seq_st, "hT", (128, KC, S), BF16)
                r_hT4 = [Res() for _ in range(4)]
                r_hT = None
                A2 = sb(seq_st, "A2", (128, 16, 1024), BF16)
                mT = A2[:].rearrange("p a b -> p (a b)").rearrange("p (k s) -> p k s", k=KC)
                r_mT = Res()
                B16 = sb(seq_st, "B16", (128, 8192), BF16)
                dgs = [B16[:, i * 4096:i * 4096 + CK * 128].rearrange("p (k j) -> p k j", k=CK) for i in range(2)]
                r_dgs = [Res(), Res()]
                Wo = B16[:].rearrange("p (k n) -> p k n", k=KC)
                r_Wo = Res()
                ssb = [sb(seq_st, f"ss{i}", (128, 4), F32) for i in range(2)]
                r_ss = [Res(), Res()]
                scopeA = ExitStack()
                YnT = sb(scopeA, "YnT", (128, 4, S), BF16)
                r_YnT = Res()
                gsl = [sb(scopeA, f"gsl{i}", (128, KC, 384), BF16) for i in range(2)]
                r_gsl = [Res(), Res()]
                wsl = [sb(scopeA, f"wsl{i}", (128, 4, 128), BF16) for i in range(2)]

                def load_u(cc):
                    sl, rs = gsl[cc % 2], r_gsl[cc % 2]
                    for part in range(2):
                        c0 = U_OFF + part * 1024 + cc * 128
                        k.dma("pool", sl[:, :, part * 128:(part + 1) * 128],
                              w_in[:, c0:c0 + 128].rearrange("(kc p) n -> p kc n", p=128), writes=[rs])
                YD = A2[:].rearrange("p a b -> p (a b)").bitcast(F32).rearrange("p (a pr s) -> p a pr s", a=2, pr=2)
                r_YD = Res()

                def normalise(quad):
                    k.emit("act", lambda e: e.activation(out=YD[:, 1], in_=YD[:, 1], func=AF.Ln), reads=[r_YD], writes=[r_YD])
                    k.emit("act", lambda e: e.activation(out=YD[:, 1], in_=YD[:, 1], func=AF.Exp, scale=-1.0),
                           reads=[r_YD], writes=[r_YD])
                    k.emit("dve", lambda e: e.tensor_tensor(out=YnT[:, 2 * quad, :], in0=YD[:, 0, 0, :],
                                                            in1=YD[:, 1, 0, :], op=ALU.mult),
                           reads=[r_YD], writes=[r_YnT])
                    k.emit("pool", lambda e: e.tensor_tensor(out=YnT[:, 2 * quad + 1, :], in0=YD[:, 0, 1, :],
                                                             in1=YD[:, 1, 1, :], op=ALU.mult),
                           reads=[r_YD], writes=[r_YnT])

                with ExitStack() as st:
                    etab = sb(st, "etab", (128, 24, 2, 128), BF16)
                    r_et = Res()
                    k.dma("sp", etab[:].rearrange("p h s q -> p (h s q)"), et_d, writes=[r_et])
                    Qpad = sb(st, "Qpad", (128, 2, 2, S), BF16)
                    r_Q = Res()
                    kT = sb(st, "kT", (128, 2, S), BF16)
                    r_kT = Res()
                    Vpad = sb(st, "Vpad", (128, NT, 4, 128), BF16)
                    r_V = Res()
                    slabs = [sb(st, f"slab{i}", (128, KC, 768), BF16) for i in range(2)]
                    r_slab = [Res(), Res()]
                    pes = [sb(st, f"pe{i}", (128, 512), BF16) for i in range(2)]
                    r_pes = [Res(), Res()]
                    pe2s = [sb(st, f"pe2{i}", (128, 512), BF16) for i in range(4)]
                    r_pe2s = [Res() for _ in range(4)]

                    def load_slab(j, g, quad):
                        sl, rs = slabs[j % 2], r_slab[j % 2]
                        for part, off in enumerate((Q_OFF, K_OFF, V_OFF)):
                            c0 = off + g * 512 + quad * 256
                            k.dma("pool", sl[:, :, part * 256:(part + 1) * 256],
                                  w_in[:, c0:c0 + 256].rearrange("(kc p) n -> p kc n", p=128), writes=[rs])

                    jobs = [(quad, g) for quad in range(2) for g in range(3)]
                    load_slab(0, jobs[0][1], jobs[0][0])
                    k.emit("pool", lambda e: e.memset(Qpad[:], 0.0), writes=[r_Q])
                    k.emit("pool", lambda e: e.memset(Vpad[:], 0.0), writes=[r_V])
                    pei_box = [0]

                    def proj_mm(sl, rs, r, tt):
                        res = []
                        for pair in range(2):
                            for which in range(2):
                                ps, rp = bank()
                                for kc in range(KC):
                                    k.emit("pe", lambda e, kc=kc, ps=ps, which=which, pair=pair: e.matmul(
                                        ps[:], lhsT=sl[:, kc, which * 256 + pair * 128: which * 256 + pair * 128 + 128],
                                        rhs=hT[:, kc, tt * 512:(tt + 1) * 512], start=(kc == 0), stop=(kc == KC - 1)),
                                        reads=[rs, r_hT4[tt]], writes=[rp], inc=(kc == KC - 1))
                                res.append((ps, rp, pair, which))
                        return res

                    def proj_ev(res, r, tt):
                        for ps, rp, pair, which in res:
                            n_i = 512 // r
                            off = n_i * tt

                            def views(p0, p1, dst_full, ps=ps, r=r, off=off, n_i=n_i):
                                if r == 1:
                                    return dst_full[p0:p1, off:off + 512], ps[p0:p1, :]
                                src = ps[p0:p1, :].rearrange("p (i c) -> p c i", c=r)
                                dst = dst_full[p0:p1, :].rearrange("p (c s) -> p c s", c=r)[:, :, off:off + n_i]
                                return dst, src
                            if which == 0:
                                d0, s0 = views(0, 64, Qpad[:, pair, 0, :])
                                d1, s1 = views(64, 128, Qpad[:, pair, 1, :])
                                k.emit("act", lambda e, d0=d0, s0=s0: e.activation(out=d0, in_=s0, func=AF.Copy),
                                       reads=[rp], writes=[r_Q])
                                k.emit("dve", lambda e, d1=d1, s1=s1: e.tensor_copy(out=d1, in_=s1),
                                       reads=[rp], writes=[r_Q])
                            else:
                                d0, s0 = views(0, 128, kT[:, pair, :])
                                k.emit("act", lambda e, d0=d0, s0=s0: e.activation(out=d0, in_=s0, func=AF.Copy),
                                       reads=[rp], writes=[r_kT])

                    def proj_qk(sl, rs, r, tt):
                        proj_ev(proj_mm(sl, rs, r, tt), r, tt)

                    b16f_p1 = B16[:].bitcast(F32)
                    xs = [b16f_p1[:, i * 1024:(i + 1) * 1024] for i in range(3)]
                    r_xs = [Res() for _ in range(3)]
                    hb = [B16[:, 6144:7168], B16[:, 7168:8192]]
                    r_hb = [Res(), Res()]
                    junk = gsl[0][:].rearrange("p a b -> p (a b)")[:, 0:D]
                    r_junk = Res()
                    load_gain(g1_d)
                    for i in range(min(2, NT)):
                        k.dma("sp", xs[i % 3][:], x_d[sq, i * 128:(i + 1) * 128, :], writes=[r_xs[i % 3]])
                    sl0, rs0 = slabs[0], r_slab[0]

                    def p1_group(gq):
                        prev = None
                        for i in range(4 * gq, 4 * gq + 4):
                            b = i % 2
                            if i + 2 < NT:
                                k.dma("sp", xs[(i + 2) % 3][:], x_d[sq, (i + 2) * 128:(i + 3) * 128, :], writes=[r_xs[(i + 2) % 3]])
                            pp = rms_front(xs[i % 3][:], r_xs[i % 3], (hb[b], r_hb[b], ssb[b], r_ss[b], junk, r_junk))
                            if prev is not None:
                                rms_back(prev[0], hT, r_hT4[gq], prev[1])
                            prev = (pp, i * 128)
                        rms_back(prev[0], hT, r_hT4[gq], prev[1])

                    p1_group(0)
                    pend = proj_mm(sl0, rs0, 1, 0)
                    for gq in range(1, 4):
                        p1_group(gq)
                        proj_ev(pend, 1, gq - 1)
                        pend = proj_mm(sl0, rs0, 1, gq)
                    proj_ev(pend, 1, 3)
                    if stage == 1:
                        k.barrier()
                        dump(hT[:, 0, :], S, r_hT4)
                        dump(hT[:, 7, :], S, r_hT4)
                        scopeA.close()
                        break

                    for j, (quad, g) in enumerate(jobs):
                        if j + 1 < len(jobs):
                            load_slab(j + 1, jobs[j + 1][1], jobs[j + 1][0])
                        sl, rs = slabs[j % 2], r_slab[j % 2]
                        r, L = GROUPS[g]
                        nb = L // 128
                        if j > 0:
                            for tt in range(4):
                                proj_qk(sl, rs, r, tt)
                        for b2 in range(NT // 2):
                            ps, rp = bank()
                            for bb in range(2):
                                blk = 2 * b2 + bb
                                c, n = blk // nb, blk % nb
                                t0 = r * 128 * n + c
                                for kc in range(KC):
                                    k.emit("pe", lambda e, kc=kc, ps=ps, bb=bb, t0=t0, r=r: e.matmul(
                                        ps[:, bb * 256:(bb + 1) * 256],
                                        lhsT=hT[:, kc, t0:t0 + 127 * r + 1:r],
                                        rhs=sl[:, kc, 512:768], start=(kc == 0), stop=(kc == KC - 1)),
                                        reads=[rs] + ([r_hT4[blk // 4]] if r == 1 else r_hT4), writes=[rp], inc=(kc == KC - 1))
                            for par in range(2):
                                src = ps[:].rearrange("p (b pr two d) -> p b pr two d", b=2, pr=2, two=2)[:, :, :, par, :]
                                dst = Vpad[:, 2 * b2:2 * b2 + 2, :, :].rearrange(
                                    "p b (pr two) d -> p b pr two d", two=2)[:, :, :, par, par * 64:(par + 1) * 64]
                                eng = "act" if par == 0 else "dve"
                                if eng == "act":
                                    k.emit("act", lambda e, dst=dst, src=src: e.activation(out=dst, in_=src, func=AF.Copy),
                                           reads=[rp], writes=[r_V])
                                else:
                                    k.emit("dve", lambda e, dst=dst, src=src: e.tensor_copy(out=dst, in_=src),
                                           reads=[rp], writes=[r_V])
                        gh0 = g * 8 + quad * 4

                        def s_stage(blk):
                            nonlocal_pei = pei_box
                            n = blk % nb
                            chunks = [(blk, 0)] + ([(blk - 1, 1)] if n > 0 else [])
                            pbs = []
                            for ci, (kb, sel) in enumerate(chunks):
                                ps, rp = bank()
                                for pr in range(2):
                                    k.emit("pe", lambda e, ps=ps, pr=pr, kb=kb, blk=blk: e.matmul(
                                        ps[:, pr * 256:(pr + 1) * 256].rearrange("p (a q) -> p a q", a=2),
                                        lhsT=kT[:, pr, kb * 128:(kb + 1) * 128],
                                        rhs=Qpad[:, pr, :, blk * 128:(blk + 1) * 128], start=True, stop=True),
                                        reads=[r_kT, r_Q], writes=[rp], inc=(pr == 1))
                                pi = nonlocal_pei[0]
                                nonlocal_pei[0] += 1
                                pb, rpb = pes[pi % 2], r_pes[pi % 2]
                                pb2, rpb2 = pe2s[pi % 4], r_pe2s[pi % 4]
                                k.emit("act", lambda e, pb=pb, ps=ps: e.activation(out=pb[:], in_=ps[:], func=AF.Exp, scale=0.125),
                                       reads=[rp], writes=[rpb])
                                k.emit("dve", lambda e, pb=pb, pb2=pb2, sel=sel: e.tensor_tensor(
                                    out=pb2[:].rearrange("p (h q) -> p h q", h=4), in0=pb[:].rearrange("p (h q) -> p h q", h=4),
                                    in1=etab[:, gh0:gh0 + 4, sel, :], op=ALU.mult),
                                    reads=[rpb, r_et], writes=[rpb2])
                                pbs.append((kb, pb2, rpb2))
                            return pbs

                        def pv_stage(blk, pbs):
                            c, n = blk // nb, blk % nb
                            pa, rpa = bank()
                            pav = pa[:].rearrange("p (a pr q) -> p a pr q", a=2, pr=2)
                            nmm = 2 * len(pbs)
                            for pr in range(2):
                                im = 0
                                for (kb, pb2, rpb2) in pbs:
                                    for par in range(2):
                                        h = 2 * pr + par
                                        k.emit("pe", lambda e, pr=pr, h=h, kb=kb, pb2=pb2, im=im, pav=pav, nmm=nmm: e.matmul(
                                            pav[:, 0, pr, :], lhsT=Vpad[:, kb, h, :], rhs=pb2[:, h * 128:(h + 1) * 128],
                                            start=(im == 0), stop=(im == nmm - 1)),
                                            reads=[r_V, rpb2], writes=[rpa], inc=(im == nmm - 1))
                                        im += 1
                            im = 0
                            for (kb, pb2, rpb2) in pbs:
                                pbv = pb2[:].rearrange("p (pr two q) -> p pr two q", pr=2, two=2)
                                for par in range(2):
                                    k.emit("pe", lambda e, par=par, pbv=pbv, im=im, pav=pav, nmm=nmm: e.matmul(
                                        pav[:, 1, :, :], lhsT=opad[:, par, :], rhs=pbv[:, :, par, :],
                                        start=(im == 0), stop=(im == nmm - 1)),
                                        reads=[r_const, rpb2], writes=[rpa], inc=(im == nmm - 1))
                                    im += 1
                            t0 = r * 128 * n + c
                            dst = YD[:, :, :, t0:t0 + 127 * r + 1:r]
                            if g == 0:
                                k.emit("dve", lambda e, dst=dst, pav=pav: e.tensor_copy(out=dst, in_=pav),
                                       reads=[rpa], writes=[r_YD])
                            else:
                                k.emit("dve", lambda e, dst=dst, pav=pav: e.tensor_tensor(out=dst, in0=pav, in1=dst, op=ALU.add),
                                       reads=[rpa, r_YD], writes=[r_YD])

                        nxt = s_stage(0)
                        for blk in range(NT):
                            cur = nxt
                            if blk + 1 < NT:
                                nxt = s_stage(blk + 1)
                            pv_stage(blk, cur)
                        if j == len(jobs) - 1:
                            load_u(0)
                            load_u(1)
                        if g == 2 and j < len(jobs) - 1:
                            normalise(quad)
                    k.barrier()
                normalise(1)
                if stage == 2:
                    for c4 in range(4):
                        dump(YnT[:, c4, :], S, [r_YnT])
                    scopeA.close()
                    break

                with scopeA as st3:
                    cT = sb(st3, "cT", (128, KC, S + 32), BF16)
                    r_cT = [Res() for _ in range(KC)]
                    lnT = cT[:, :, 0:S]
                    r_lnT = [Res() for _ in range(4)]
                    k.emit("pool", lambda e: e.memset(cT[:, :, 0:32], 0.0), writes=r_cT)
                    gg = [sb(st3, f"gg{i}", (128, 2, 512), F32) for i in range(2)]
                    r_gg = [Res(), Res()]
                    cvs = [sb(st3, f"cv{i}", (128, KC, 512), F32) for i in range(2)]
                    r_cvs = [[Res() for _ in range(KC)] for _ in range(2)]
                    cvb = [sb(st3, f"cvb{i}", (128, 512), BF16) for i in range(2)]
                    r_cvb = [Res(), Res()]
                    cvq = [sb(st3, f"cvq{i}", (128, 512), BF16) for i in range(2)]
                    r_cvq = [Res(), Res()]
                    stt = sb(st3, "stt", (128, 4, 512), F32)
                    r_st = Res()

                    def load_g(oc):
                        sl, rs = gsl[oc % 2], r_gsl[oc % 2]
                        for part in range(2):
                            c0 = G_OFF + part * 1024 + oc * 128
                            k.dma("pool", sl[:, :, part * 128:(part + 1) * 128],
                                  w_in[:, c0:c0 + 128].rearrange("(kc p) n -> p kc n", p=128), writes=[rs])
                        k.dma("pool", sl[:, :, 256:384],
                              w_co[:, oc * 128:(oc + 1) * 128].rearrange("(kc p) n -> p kc n", p=128), writes=[rs])
                        k.dma("pool", wsl[oc % 2][:],
                              w_ao[:, oc * 128:(oc + 1) * 128].rearrange("(kc p) n -> p kc n", p=128), writes=[rs])

                    si = 0
                    for cc in range(KC):
                        if 1 <= cc and cc + 1 < KC:
                            load_u(cc + 1)
                        sl, rs = gsl[cc % 2], r_gsl[cc % 2]
                        for tt in range(4):
                            pA, rA = bank()
                            pB, rB = bank()
                            for part, (pp, rr) in enumerate(((pA, rA), (pB, rB))):
                                for kc in range(KC):
                                    k.emit("pe", lambda e, kc=kc, pp=pp, part=part, sl=sl, tt=tt: e.matmul(
                                        pp[:], lhsT=sl[:, kc, part * 128:(part + 1) * 128],
                                        rhs=hT[:, kc, tt * 512:(tt + 1) * 512], start=(kc == 0), stop=(kc == KC - 1)),
                                        reads=[rs, r_hT4[tt]], writes=[rr], inc=(kc == KC - 1))
                            sg, rsg = gg[si % 2], r_gg[si % 2]
                            si += 1
                            k.emit("act", lambda e, sg=sg, pB=pB: e.activation(out=sg[:, 0, :], in_=pB[:], func=AF.Sigmoid),
                                   reads=[rB], writes=[rsg])
                            k.emit("dve", lambda e, sg=sg, pA=pA, cc=cc, tt=tt: e.tensor_tensor(
                                out=cT[:, cc, 32 + tt * 512:32 + (tt + 1) * 512], in0=pA[:], in1=sg[:, 0, :], op=ALU.mult),
                                reads=[rA, rsg], writes=[r_cT[cc]])
                    load_g(0)
                    load_g(1)
                    if stage == 3:
                        k.barrier()
                        dump(cT[:, 0, 32:32 + S], S, [r_cT[0]])
                        dump(cT[:, 7, 32:32 + S], S, [r_cT[7]])
                        break

                    built = set()

                    def build_dg(j):
                        if j in built or j >= 4 * KC:
                            return
                        built.add(j)
                        cc_ = j % KC
                        for eng, t0_, t1_ in (("dve", 0, 15), ("pool", 15, CK)):
                            nt_ = t1_ - t0_
                            k.emit(eng, lambda e, j=j, cc_=cc_, t0_=t0_, t1_=t1_, nt_=nt_: e.tensor_tensor(
                                out=dgs[j % 2][:, t0_:t1_, :], in0=identf[:].unsqueeze(1).to_broadcast([128, nt_, 128]),
                                in1=cw[:, cc_, t0_:t1_].unsqueeze(2).to_broadcast([128, nt_, 128]), op=ALU.mult),
                                reads=[r_const], writes=[r_dgs[j % 2]])
                    build_dg(0)
                    bank_n[0] = 4
                    pend_stats = []

                    def flush_stats():
                        while pend_stats:
                            b_, cc_ = pend_stats.pop(0)
                            k.emit("pe", lambda e, b_=b_, cc_=cc_: e.matmul(banks[4][:], lhsT=onesm[:], rhs=cvb[b_][:],
                                                                            start=(cc_ == 0), stop=(cc_ == KC - 1)),
                                   reads=[r_cvb[b_], r_const], writes=[bres[4]])
                            k.emit("pe", lambda e, b_=b_, cc_=cc_: e.matmul(banks[5][:], lhsT=onesm[:], rhs=cvq[b_][:],
                                                                            start=(cc_ == 0), stop=(cc_ == KC - 1)),
                                   reads=[r_cvq[b_], r_const], writes=[bres[5]])
                    def z_phase(tz, chunks=tuple(range(KC))):
                        cvz, r_cvz = cvs[tz % 2], r_cvs[tz % 2]
                        for cc in chunks:
                            zeng = "pool" if cc in (1, 5) else "dve"
                            k.emit(zeng, lambda e, cc=cc: e.tensor_tensor(out=cvz[:, cc, :], in0=cvz[:, cc, :], in1=stt[:, 2, :], op=ALU.mult),
                                   reads=[r_cvz[cc], r_st], writes=[r_cvz[cc]])
                            k.emit(zeng, lambda e, cc=cc: e.tensor_tensor(out=cvz[:, cc, :], in0=cvz[:, cc, :], in1=stt[:, 3, :], op=ALU.add),
                                   reads=[r_cvz[cc], r_st], writes=[r_cvz[cc]])
                            k.emit("act", lambda e, cc=cc: e.activation(out=lnT[:, cc, tz * 512:(tz + 1) * 512], in_=cvz[:, cc, :], func=AF.Silu,
                                                                        scale=cvec[:, 1, cc:cc + 1], bias=cvec[:, 2, cc:cc + 1]),
                                   reads=[r_cvz[cc], r_const], writes=[r_lnT[tz]])

                    for tt in range(4):
                        cv, r_cv = cvs[tt % 2], r_cvs[tt % 2]
                        pM1, rM1 = banks[4], bres[4]
                        pM2, rM2 = banks[5], bres[5]
                        for cc in range(KC):
                            jj = tt * KC + cc
                            dg, r_dg = dgs[jj % 2], r_dgs[jj % 2]
                            build_dg(jj + 1)
                            if tt > 0 and 2 <= cc <= 5:
                                z_phase(tt - 1, (2 * (cc - 2), 2 * (cc - 2) + 1))
                            ps, rp = bank()
                            for kk in range(CK):
                                c0 = 2 + tt * 512 + kk
                                k.emit("pe", lambda e, kk=kk, ps=ps, cc=cc, c0=c0, dg=dg: e.matmul(
                                    ps[:], lhsT=dg[:, kk, :], rhs=cT[:, cc, c0:c0 + 512], start=(kk == 0), stop=(kk == CK - 1)),
                                    reads=[r_dg, r_cT[cc]], writes=[rp], inc=(kk == CK - 1))
                            flush_stats()
                            k.emit("act", lambda e, ps=ps, cc=cc: e.activation(out=cv[:, cc, :], in_=ps[:], func=AF.Identity,
                                                                               bias=cvec[:, 0, cc:cc + 1]),
                                   reads=[rp, r_const], writes=[r_cv[cc]])
                            b = cc % 2
                            k.emit("act", lambda e, cc=cc, b=b: e.activation(out=cvb[b][:], in_=cv[:, cc, :], func=AF.Copy),
                                   reads=[r_cv[cc]], writes=[r_cvb[b]])
                            k.emit("act", lambda e, cc=cc, b=b: e.activation(out=cvq[b][:], in_=cv[:, cc, :], func=AF.Square),
                                   reads=[r_cv[cc]], writes=[r_cvq[b]])
                            pend_stats.append((b, cc))
                        build_dg(tt * KC + KC + 1)
                        flush_stats()
                        k.emit("act", lambda e, pM1=pM1: e.activation(out=stt[:, 0, :], in_=pM1[:], func=AF.Copy),
                               reads=[rM1], writes=[r_st])
                        k.emit("dve", lambda e: e.tensor_tensor(out=stt[:, 1, :], in0=stt[:, 0, :], in1=stt[:, 0, :], op=ALU.mult),
                               reads=[r_st], writes=[r_st])
                        k.emit("dve", lambda e, pM2=pM2: e.tensor_tensor(out=stt[:, 1, :], in0=pM2[:], in1=stt[:, 1, :], op=ALU.subtract),
                               reads=[rM2, r_st], writes=[r_st])
                        k.emit("act", lambda e: e.activation(out=stt[:, 1, :], in_=stt[:, 1, :], func=AF.Sqrt, bias=epsr[:, 1:2]),
                               reads=[r_st, r_const], writes=[r_st])
                        k.emit("dve", lambda e: e.reciprocal(out=stt[:, 2, :], in_=stt[:, 1, :]), reads=[r_st], writes=[r_st])
                        k.emit("dve", lambda e: e.scalar_tensor_tensor(out=stt[:, 3, :], in0=stt[:, 0, :], scalar=-1.0, in1=stt[:, 2, :],
                                                                       op0=ALU.mult, op1=ALU.mult), reads=[r_st], writes=[r_st])
                    bank_n[0] = NBK

                    ggi = 0
                    for oc in range(KC):
                        if 1 <= oc and oc + 1 < KC:
                            load_g(oc + 1)
                        k.dma("pool", Wo[:, oc, :], w_o[oc * 128:(oc + 1) * 128, :], writes=[r_Wo, r_dgs[0], r_dgs[1]])
                        sl, rs, ws = gsl[oc % 2], r_gsl[oc % 2], wsl[oc % 2]
                        for tt in range(4):
                            if oc == 0 and tt == 2:
                                z_phase(3)
                            p1, rp1 = bank()
                            p3, rp3 = bank()
                            p4, rp4 = bank()
                            p2, rp2 = bank()
                            for kc in range(4):
                                k.emit("pe", lambda e, kc=kc, p1=p1, ws=ws, tt=tt: e.matmul(
                                    p1[:], lhsT=ws[:, kc, :], rhs=YnT[:, kc, tt * 512:(tt + 1) * 512],
                                    start=(kc == 0), stop=(kc == 3)), reads=[rs, r_YnT], writes=[rp1], inc=(kc == 3))
                            for part, (pp, rr) in enumerate(((p3, rp3), (p4, rp4))):
                                for kc in range(KC):
                                    k.emit("pe", lambda e, kc=kc, pp=pp, part=part, sl=sl, tt=tt: e.matmul(
                                        pp[:], lhsT=sl[:, kc, part * 128:(part + 1) * 128], rhs=hT[:, kc, tt * 512:(tt + 1) * 512],
                                        start=(kc == 0), stop=(kc == KC - 1)), reads=[rs, r_hT4[tt]], writes=[rr], inc=(kc == KC - 1))
                            for kc in range(KC):
                                k.emit("pe", lambda e, kc=kc, p2=p2, sl=sl, tt=tt: e.matmul(
                                    p2[:], lhsT=sl[:, kc, 256:384], rhs=lnT[:, kc, tt * 512:(tt + 1) * 512],
                                    start=(kc == 0), stop=(kc == KC - 1)), reads=[rs, r_lnT[tt]], writes=[rp2], inc=(kc == KC - 1))
                            g2, rg2 = gg[ggi % 2], r_gg[ggi % 2]
                            ggi += 1
                            k.emit("act", lambda e, g2=g2, p3=p3, oc=oc: e.activation(out=g2[:, 0, :], in_=p3[:], func=AF.Sigmoid,
                                                                                      bias=gatb[:, oc:oc + 1]),
                                   reads=[rp3, r_const], writes=[rg2])
                            k.emit("act", lambda e, g2=g2, p4=p4, oc=oc: e.activation(out=g2[:, 1, :], in_=p4[:], func=AF.Sigmoid,
                                                                                      bias=gatb[:, 8 + oc:9 + oc]),
                                   reads=[rp4, r_const], writes=[rg2])
                            k.emit("dve", lambda e, g2=g2, p1=p1: e.tensor_tensor(out=g2[:, 0, :], in0=p1[:], in1=g2[:, 0, :], op=ALU.mult),
                                   reads=[rp1, rg2], writes=[rg2])
                            k.emit("dve", lambda e, g2=g2, p2=p2: e.tensor_tensor(out=g2[:, 1, :], in0=p2[:], in1=g2[:, 1, :], op=ALU.mult),
                                   reads=[rp2, rg2], writes=[rg2])
                            k.emit("dve", lambda e, g2=g2, oc=oc, tt=tt: e.tensor_tensor(out=mT[:, oc, tt * 512:(tt + 1) * 512], in0=g2[:, 0, :],
                                                                                         in1=g2[:, 1, :], op=ALU.add),
                                   reads=[rg2], writes=[r_mT])
                    k.barrier()
                if stage == 4:
                    dump(mT[:, 0, :], S, [r_mT])
                    dump(mT[:, 7, :], S, [r_mT])
                    break

                with ExitStack() as scopeB:
                    x1 = sb(scopeB, "x1", (128, NT, D), F32)
                    r_x1 = [Res() for _ in range(NT)]
                    h2T, r_h2T = hT, Res()
                    NSLOT = 30
                    NPRE = NSLOT - NF
                    Wd = sb(scopeB, "Wd", (128, NSLOT, 512), BF16)
                    r_Wd = [Res() for _ in range(NSLOT)]

                    def wslot(h, f):
                        return (NF * h + f) % NSLOT
                    fsl = [sb(scopeB, f"fsl{i}", (128, KC, 256), BF16) for i in range(3)]
                    r_fsl = [Res(), Res(), Res()]

                    def load_wd(h, f_lo, f_hi):
                        half = h % 2
                        for f in range(f_lo, f_hi):
                            k.dma("pool", Wd[:, wslot(h, f), :], w_fd[f * 128:(f + 1) * 128, half * 512:(half + 1) * 512],
                                  writes=[r_Wd[wslot(h, f)]])

                    def load_f(jj, f):
                        sl, rs = fsl[jj % 3], r_fsl[jj % 3]
                        for part, wsrc in enumerate((w_fg, w_fu)):
                            k.dma("pool", sl[:, :, part * 128:(part + 1) * 128],
                                  wsrc[:, f * 128:(f + 1) * 128].rearrange("(kc p) n -> p kc n", p=128), writes=[rs])
                    with ExitStack() as st:
                        xs = [sb(st, f"xsb{i}", (128, D), F32) for i in range(2)]
                        r_xs = [Res(), Res()]
                        hb = [sb(st, f"hbb{i}", (128, D), BF16) for i in range(2)]
                        r_hb = [Res(), Res()]
                        ssb = [sb(st, f"ssb{i}", (128, 4), F32) for i in range(2)]
                        r_ss = [Res(), Res()]
                        junk = sb(st, "junkb", (128, D), BF16)
                        r_junk = Res()
                        load_gain(g2_d)
                        prev = None

                        def wo_mm(i):
                            res = []
                            for half in range(2):
                                ps, rp = bank()
                                for kc in range(KC):
                                    k.emit("pe", lambda e, kc=kc, ps=ps, half=half, i=i: e.matmul(
                                        ps[:], lhsT=mT[:, kc, i * 128:(i + 1) * 128], rhs=Wo[:, kc, half * 512:(half + 1) * 512],
                                        start=(kc == 0), stop=(kc == KC - 1)), reads=[r_mT, r_Wo], writes=[rp], inc=(kc == KC - 1))
                                res.append((ps, rp))
                            return res
                        k.dma("sp", xs[0][:], x_d[sq, 0:128, :], writes=[r_xs[0]])
                        nxt = wo_mm(0)
                        for i in range(NT):
                            b = i % 2
                            cur = nxt
                            if i == 3:
                                load_f(0, 0)
                                load_f(1, 1)
                                load_wd(0, 0, NF)
                            if i + 1 < NT:
                                k.dma("sp", xs[(i + 1) % 2][:], x_d[sq, (i + 1) * 128:(i + 2) * 128, :], writes=[r_xs[(i + 1) % 2]])
                                nxt = wo_mm(i + 1)
                            for half in range(2):
                                ps, rp = cur[half]
                                k.emit("dve", lambda e, ps=ps, half=half, i=i, b=b: e.tensor_tensor(
                                    out=x1[:, i, half * 512:(half + 1) * 512], in0=ps[:], in1=xs[b][:, half * 512:(half + 1) * 512], op=ALU.add),
                                    reads=[rp, r_xs[b]], writes=[r_x1[i]])
                            pp = rms_front(x1[:, i, :], r_x1[i], (hb[b], r_hb[b], ssb[b], r_ss[b], junk, r_junk))
                            if prev is not None:
                                rms_back(prev[0], h2T, r_h2T, prev[1])
                            prev = (pp, i * 128)
                        rms_back(prev[0], h2T, r_h2T, prev[1])
                        k.barrier()
                    if stage == 5:
                        dump(x1[:, 0, :], D, [r_x1[0]])
                        dump(x1[:, 15, :], D, [r_x1[15]])
                        dump(h2T[:, 0, :], S, [r_h2T])
                        break

                    with ExitStack() as st:
                        ST = 1024
                        aTb = sb(st, "aTb", (128, NF - 16, ST), BF16)
                        r_aT = Res()

                        def aT(f):
                            return A2[:, f, :] if f < 16 else aTb[:, f - 16, :]
                        b16f = B16[:].bitcast(F32)
                        sgs = [b16f[:, 2048 + i * 512:2048 + (i + 1) * 512] for i in range(2)]
                        r_sg = [Res(), Res()]
                        ssb = [sb(st, f"fss{i}", (128, 4), F32) for i in range(2)]
                        r_ss = [Res(), Res()]
                        junk = B16[:, 6144:7168]
                        r_junk = Res()
                        xo = [b16f[:, i * 1024:(i + 1) * 1024] for i in range(2)]
                        r_xo = [Res(), Res()]
                        load_gain(gf_d)

                        fj = 0
                        si = 0
                        for sti in range(S // ST):
                            for f in range(NF):
                                if f + 2 < NF:
                                    load_f(fj + 2, f + 2)
                                sl, rs = fsl[fj % 3], r_fsl[fj % 3]
                                fj += 1
                                for half in range(ST // 512):
                                    t0 = sti * ST + half * 512
                                    pG, rG = bank()
                                    pU, rU = bank()
                                    for part, (pp, rr) in enumerate(((pG, rG), (pU, rU))):
                                        for kc in range(KC):
                                            k.emit("pe", lambda e, kc=kc, pp=pp, part=part, sl=sl, t0=t0: e.matmul(
                                                pp[:], lhsT=sl[:, kc, part * 128:(part + 1) * 128], rhs=h2T[:, kc, t0:t0 + 512],
                                                start=(kc == 0), stop=(kc == KC - 1)), reads=[rs, r_h2T], writes=[rr], inc=(kc == KC - 1))
                                    sg, rsg = sgs[si % 2], r_sg[si % 2]
                                    si += 1
                                    k.emit("act", lambda e, sg=sg, pG=pG: e.activation(out=sg[:], in_=pG[:], func=AF.Silu),
                                           reads=[rG], writes=[rsg])
                                    k.emit("dve", lambda e, sg=sg, pU=pU, f=f, half=half: e.tensor_tensor(
                                        out=aT(f)[:, half * 512:(half + 1) * 512], in0=pU[:], in1=sg[:], op=ALU.mult),
                                        reads=[rU, rsg], writes=[r_aT])
                            if sti + 1 < S // ST:
                                load_f(fj, 0)
                                load_f(fj + 1, 1)
                            for half in range(2):
                                hg = sti * 2 + half
                                nh = 2 * (S // ST)
                                if hg + 1 < nh:
                                    load_wd(hg + 1, 0, NPRE)
                                for il in range(ST // 128):
                                    i = sti * (ST // 128) + il
                                    b = i % 2
                                    ps, rp = bank()
                                    last_tile = (il == ST // 128 - 1)
                                    for f in range(NF):
                                        k.emit("pe", lambda e, f=f, ps=ps, il=il, hg=hg: e.matmul(
                                            ps[:], lhsT=aT(f)[:, il * 128:(il + 1) * 128], rhs=Wd[:, wslot(hg, f), :],
                                            start=(f == 0), stop=(f == NF - 1)), reads=[r_aT, r_Wd[wslot(hg, f)]], writes=[rp],
                                            inc=(f == NF - 1 or last_tile))
                                    if last_tile and hg + 1 < nh:
                                        load_wd(hg + 1, NPRE, NF)
                                    k.emit("dve", lambda e, ps=ps, half=half, i=i: e.tensor_tensor(
                                        out=x1[:, i, half * 512:(half + 1) * 512], in0=ps[:], in1=x1[:, i, half * 512:(half + 1) * 512], op=ALU.add),
                                        reads=[rp, r_x1[i]], writes=[r_x1[i]])
                                    if half == 0:
                                        continue
                                    ss, rss = ssb[b], r_ss[b]
                                    k.emit("act", lambda e, i=i, ss=ss: e.activation(out=junk[:], in_=x1[:, i, :], func=AF.Square, accum_out=ss[:, 0:1]),
                                           reads=[r_x1[i]], writes=[r_junk, rss])
                                    k.emit("act", lambda e, ss=ss: e.activation(out=ss[:, 1:2], in_=ss[:, 0:1], func=AF.Sqrt, scale=1.0 / D,
                                                                                bias=epsr[:, 0:1]), reads=[rss, r_const], writes=[rss])
                                    k.emit("dve", lambda e, ss=ss: e.reciprocal(out=ss[:, 2:3], in_=ss[:, 1:2]), reads=[rss], writes=[rss])
                                    k.emit("dve", lambda e, ss=ss, i=i, b=b: e.scalar_tensor_tensor(
                                        out=xo[b][:], in0=x1[:, i, :], scalar=ss[:, 2:3], in1=gbc[:], op0=ALU.mult, op1=ALU.mult),
                                        reads=[r_x1[i], rss, r_gbc], writes=[r_xo[b]])
                                    k.dma("sp", out_d[sq, i * 128:(i + 1) * 128, :], xo[b][:], reads=[r_xo[b]])
                        k.barrier()
        k.barrier()
    return nc


def _host_inputs(x, norm1_g, w_in, gate_b, conv_w, conv_b, conv_ln_g, conv_ln_b, w_conv_out, w_attn_out, w_o,
                 norm2_g, w_ffn_gate, w_ffn_up, w_ffn_down, norm_f_g):
    f = lambda a: np.ascontiguousarray(np.asarray(a, dtype=np.float32))
    common = {
        "w_in": f(w_in[0]), "w_conv_out": f(w_conv_out[0]), "w_attn_out": f(w_attn_out[0]), "w_o": f(w_o[0]),
        "w_ffn_gate": f(w_ffn_gate[0]), "w_ffn_up": f(w_ffn_up[0]), "w_ffn_down": f(w_ffn_down[0]),
        "g1": f(norm1_g[0]).reshape(1, D), "g2": f(norm2_g[0]).reshape(1, D), "gf": f(norm_f_g).reshape(1, D),
        "gate_b": f(np.asarray(gate_b[0]).reshape(16, 128).T),
        "conv_w": f(np.asarray(conv_w[0]).reshape(CK, KC, 128).transpose(2, 1, 0).reshape(128, KC * CK)),
        "cvec": f(np.stack([np.asarray(v[0]).reshape(KC, 128).T for v in (conv_b, conv_ln_g, conv_ln_b)], axis=1).reshape(128, 3 * KC)),
        "etab": _etable().astype(ml_dtypes.bfloat16),
        "ident": np.eye(128, dtype=np.float32),
        "identb": np.eye(128, dtype=np.float32).astype(ml_dtypes.bfloat16),
    }
    xx = f(x)
    return [dict(common, x=np.ascontiguousarray(xx[2 * c:2 * c + 2])) for c in range(8)]


def kernel(**inputs):
    in_maps = _host_inputs(**inputs)
    nc = build_nc()
    res = run_bass_kernel_spmd(nc, in_maps, core_ids=list(range(8)))
    return np.concatenate([r["out"] for r in res.results], axis=0).astype(np.float32)
```
